# Optimizing a Trainium2 kernel written in Bass

```python
import math
import jax, jax.numpy as jnp
from jax import lax
import numpy as np

D_MODEL = 2048
BATCH = 8
SEQ = 2048
DEPTH = 2

CHUNK = 64
N_EVEN = (DEPTH + 1) // 2
N_ODD = DEPTH // 2

D_CONV = D_MODEL // 2
D_SSM = D_MODEL // 2
CONV_WIDTH = 31
SSM_GROUP = 16
N_SSM_GROUPS = D_SSM // SSM_GROUP
SSM_STATE = 64
D_IN = 2 * D_CONV + D_SSM

N_HEADS = 16
HEAD_DIM = D_MODEL // N_HEADS
Q_BLOCK = 128

D_FF = 5632
FFN_CONV_WIDTH = 3

EPS = 1e-6

kernel_name = "hybrid_conformer_s5_stickbreak_convffn"


def rmsnorm(x, g):
    xf = x.astype(jnp.float32)
    y = xf * lax.rsqrt(jnp.mean(xf * xf, axis=-1, keepdims=True) + EPS)
    return (y * g.astype(jnp.float32)).astype(x.dtype)


def causal_depthwise_conv(x, w):
    k_width, channels = w.shape
    xp = jnp.pad(x, ((0, 0), (k_width - 1, 0), (0, 0)))
    return lax.conv_general_dilated(
        xp, w.astype(x.dtype)[:, None, :], window_strides=(1,), padding="VALID",
        dimension_numbers=("NWC", "WIO", "NWC"), feature_group_count=channels)


def conformer_conv(u_val, u_gate, w_dw, b_dw, ln_g, ln_b):
    h = u_val * jax.nn.sigmoid(u_gate)
    h = causal_depthwise_conv(h, w_dw) + b_dw.astype(h.dtype)
    hf = h.astype(jnp.float32)
    mu = jnp.mean(hf, axis=-1, keepdims=True)
    var = jnp.mean(jnp.square(hf - mu), axis=-1, keepdims=True)
    hf = (hf - mu) * lax.rsqrt(var + EPS) * ln_g.astype(jnp.float32) + ln_b.astype(jnp.float32)
    return jax.nn.silu(hf).astype(u_val.dtype)


def s5_mixer(u, lam_re, lam_im, log_step, b_re, b_im, c_re, c_im, d_skip, w_glu, b_glu):
    bsz, seq_len, _ = u.shape
    f32 = jnp.float32
    uf = u.astype(f32).reshape(bsz, seq_len, N_SSM_GROUPS, SSM_GROUP)
    lr, li = lam_re.astype(f32), lam_im.astype(f32)
    step = jnp.exp(log_step.astype(f32))[:, None]
    mag = jnp.exp(lr * step)
    lbar_re = mag * jnp.cos(li * step)
    lbar_im = mag * jnp.sin(li * step)
    num_re, num_im = lbar_re - 1.0, lbar_im
    den = lr * lr + li * li
    f_re = (num_re * lr + num_im * li) / den
    f_im = (num_im * lr - num_re * li) / den
    br, bi = b_re.astype(f32), b_im.astype(f32)
    bb_re = f_re[..., None] * br - f_im[..., None] * bi
    bb_im = f_re[..., None] * bi + f_im[..., None] * br
    bu_re = jnp.einsum("blgh,gph->blgp", uf, bb_re)
    bu_im = jnp.einsum("blgh,gph->blgp", uf, bb_im)
    a_re = jnp.broadcast_to(lbar_re, (1, seq_len, N_SSM_GROUPS, SSM_STATE))
    a_im = jnp.broadcast_to(lbar_im, (1, seq_len, N_SSM_GROUPS, SSM_STATE))

    def combine(earlier, later):
        ar1, ai1, xr1, xi1 = earlier
        ar2, ai2, xr2, xi2 = later
        ar = ar2 * ar1 - ai2 * ai1
        ai = ar2 * ai1 + ai2 * ar1
        xr = ar2 * xr1 - ai2 * xi1 + xr2
        xi = ar2 * xi1 + ai2 * xr1 + xi2
        return (ar, ai, xr, xi)

    _, _, s_re, s_im = lax.associative_scan(combine, (a_re, a_im, bu_re, bu_im), axis=1)
    y = (jnp.einsum("blgp,ghp->blgh", s_re, c_re.astype(f32))
         - jnp.einsum("blgp,ghp->blgh", s_im, c_im.astype(f32))
         + d_skip.astype(f32) * uf)
    y = jax.nn.gelu(y.reshape(bsz, seq_len, D_SSM))
    y = y * jax.nn.sigmoid(y @ w_glu.astype(f32) + b_glu.astype(f32))
    return y.astype(u.dtype)


def stick_breaking_attention(q, k, v):
    seq_len = q.shape[2]
    scale = HEAD_DIM ** -0.5
    outs = []
    for start in range(0, seq_len, Q_BLOCK):
        end = start + Q_BLOCK
        qb = q[:, :, start:end]
        kb = k[:, :, :end]
        vb = v[:, :, :end]
        z = jnp.einsum("bhqd,bhkd->bhqk", qb, kb).astype(jnp.float32) * scale
        t_idx = start + jnp.arange(Q_BLOCK)[:, None]
        s_idx = jnp.arange(end)[None, :]
        before = s_idx < t_idx
        log_beta = jax.nn.log_sigmoid(z)
        log_keep = jnp.where(before, jax.nn.log_sigmoid(-z), 0.0)
        suffix = lax.cumsum(log_keep, axis=3, reverse=True) - log_keep
        w = jnp.where(before, jnp.exp(log_beta + suffix), 0.0)
        outs.append(jnp.einsum("bhqk,bhkd->bhqd", w.astype(v.dtype), vb))
    return jnp.concatenate(outs, axis=2)


def conv_ffn(x, w_up, w_dw, w_down):
    u = x @ w_up
    u = causal_depthwise_conv(u, w_dw)
    gate, val = u[..., :D_FF], u[..., D_FF:]
    return (jax.nn.silu(gate) * val) @ w_down


def setup_inputs(seed: int = 0) -> dict:
    key = jax.random.key(seed)
    ks = jax.random.split(key, 32)
    f32 = jnp.float32
    nrm = lambda k, shape, s: jax.random.normal(k, shape, f32) * s
    gain = lambda k, shape: 1.0 + 0.02 * jax.random.normal(k, shape, f32)
    G, P, H = N_SSM_GROUPS, SSM_STATE, SSM_GROUP
    n_idx = jnp.arange(P, dtype=f32)
    return {
        "x": jax.random.normal(ks[0], (BATCH, SEQ, D_MODEL), f32),
        "ln_mix_even": gain(ks[1], (N_EVEN, D_MODEL)),
        "w_in": nrm(ks[2], (N_EVEN, D_MODEL, D_IN), D_MODEL ** -0.5),
        "conv_w": nrm(ks[3], (N_EVEN, CONV_WIDTH, D_CONV), CONV_WIDTH ** -0.5),
        "conv_b": nrm(ks[4], (N_EVEN, D_CONV), 0.02),
        "conv_ln_g": gain(ks[5], (N_EVEN, D_CONV)),
        "conv_ln_b": nrm(ks[6], (N_EVEN, D_CONV), 0.02),
        "ssm_lam_re": -0.5 + 0.01 * jax.random.normal(ks[7], (N_EVEN, G, P), f32),
        "ssm_lam_im": jnp.pi * n_idx + 0.01 * jax.random.normal(ks[8], (N_EVEN, G, P), f32),
        "ssm_log_step": jax.random.uniform(ks[9], (N_EVEN, G), f32, math.log(1e-3), math.log(1e-1)),
        "ssm_b_re": nrm(ks[10], (N_EVEN, G, P, H), (2 * H) ** -0.5),
        "ssm_b_im": nrm(ks[11], (N_EVEN, G, P, H), (2 * H) ** -0.5),
        "ssm_c_re": nrm(ks[12], (N_EVEN, G, H, P), P ** -0.5),
        "ssm_c_im": nrm(ks[13], (N_EVEN, G, H, P), P ** -0.5),
        "ssm_d": nrm(ks[14], (N_EVEN, G, H), 1.0),
        "ssm_w_glu": nrm(ks[15], (N_EVEN, D_SSM, D_SSM), D_SSM ** -0.5),
        "ssm_b_glu": nrm(ks[16], (N_EVEN, D_SSM), 0.02),
        "w_out_even": nrm(ks[17], (N_EVEN, D_CONV + D_SSM, D_MODEL), (D_CONV + D_SSM) ** -0.5),
        "ln_mix_odd": gain(ks[18], (N_ODD, D_MODEL)),
        "w_qkv": nrm(ks[19], (N_ODD, D_MODEL, 3 * D_MODEL), D_MODEL ** -0.5),
        "w_o": nrm(ks[20], (N_ODD, D_MODEL, D_MODEL), D_MODEL ** -0.5),
        "ln_ffn": gain(ks[21], (DEPTH, D_MODEL)),
        "ffn_w_up": nrm(ks[22], (DEPTH, D_MODEL, 2 * D_FF), D_MODEL ** -0.5),
        "ffn_conv_w": nrm(ks[23], (DEPTH, FFN_CONV_WIDTH, 2 * D_FF), FFN_CONV_WIDTH ** -0.5),
        "ffn_w_down": nrm(ks[24], (DEPTH, D_FF, D_MODEL), D_FF ** -0.5),
        "ln_final": gain(ks[25], (D_MODEL,)),
    }


def reference(x, ln_mix_even, w_in, conv_w, conv_b, conv_ln_g, conv_ln_b,
              ssm_lam_re, ssm_lam_im, ssm_log_step, ssm_b_re, ssm_b_im,
              ssm_c_re, ssm_c_im, ssm_d, ssm_w_glu, ssm_b_glu, w_out_even,
              ln_mix_odd, w_qkv, w_o,
              ln_ffn, ffn_w_up, ffn_conv_w, ffn_w_down, ln_final):
    bsz, seq_len, _ = x.shape
    for layer in range(DEPTH):
        i = layer // 2
        if layer % 2 == 0:
            h = rmsnorm(x, ln_mix_even[i])
            u = h @ w_in[i]
            ya = conformer_conv(u[..., :D_CONV], u[..., D_CONV:2 * D_CONV],
                                conv_w[i], conv_b[i], conv_ln_g[i], conv_ln_b[i])
            yb = s5_mixer(u[..., 2 * D_CONV:], ssm_lam_re[i], ssm_lam_im[i], ssm_log_step[i],
                          ssm_b_re[i], ssm_b_im[i], ssm_c_re[i], ssm_c_im[i], ssm_d[i],
                          ssm_w_glu[i], ssm_b_glu[i])
            x = x + jnp.concatenate([ya, yb], axis=-1) @ w_out_even[i]
        else:
            h = rmsnorm(x, ln_mix_odd[i])
            qkv = (h @ w_qkv[i]).reshape(bsz, seq_len, 3, N_HEADS, HEAD_DIM)
            qkv = jnp.transpose(qkv, (2, 0, 3, 1, 4))
            o = stick_breaking_attention(qkv[0], qkv[1], qkv[2])
            o = jnp.transpose(o, (0, 2, 1, 3)).reshape(bsz, seq_len, D_MODEL)
            x = x + o @ w_o[i]
        h = rmsnorm(x, ln_ffn[layer])
        x = x + conv_ffn(h, ffn_w_up[layer], ffn_conv_w[layer], ffn_w_down[layer])
    return rmsnorm(x, ln_final)
```

```python
import math
import numpy as np
import concourse.bass as bass
import concourse.mybir as mybir
from concourse.bass_utils import run_bass_kernel_spmd

F32 = mybir.dt.float32
BF16 = mybir.dt.bfloat16
U8 = mybir.dt.uint8
I32 = mybir.dt.int32
TWO_PI_S = 6.28318
AF = mybir.ActivationFunctionType
ALU = mybir.AluOpType

L = 2048
D = 2048
TW = 512
NT = L // TW
DFF = 5632
NPAIR = DFF // 128
EPS = 1e-6
ARENA = 205 * 1024
NDS = 24

C_ID, C_ONES, C_TRI, C_ONEG, C_MASK, C_RM, C_TV = 0, 128, 256, 384, 512, 2560, 2568
C_TOT = 2568 + 2048


def make_consts():
    c = np.zeros((128, C_TOT), np.float32)
    c[:, C_ID:C_ID + 128] = np.eye(128, dtype=np.float32)
    c[:, C_ONES:C_ONES + 128] = 1.0
    j = np.arange(128)[:, None]
    s = np.arange(128)[None, :]
    c[:, C_TRI:C_TRI + 128] = -(j >= s).astype(np.float32)
    c[:, C_ONEG:C_ONEG + 128] = -1.0
    cc = np.arange(512)[None, :]
    for i in range(4):
        c[:, C_MASK + 512 * i:C_MASK + 512 * (i + 1)] = np.where((128 * i + j) < cc, 0.0, -100.0).astype(np.float32)
    c[:, C_RM:C_RM + 8] = ((j // 16) == np.arange(8)[None, :]).astype(np.float32)
    c[:, C_TV:C_TV + 2048] = np.arange(2048, dtype=np.float32)[None, :]
    return c


class Buf:
    __slots__ = ("w", "r", "name")

    def __init__(self, name=""):
        self.w = {}
        self.r = {}
        self.name = name


class Tile:
    __slots__ = ("ap", "b")

    def __init__(self, ap, b):
        self.ap = ap
        self.b = b


class Eng:
    def __init__(self, nc, name, h):
        self.name = name
        self.h = h
        self.sem = nc.alloc_semaphore("es_" + name)
        self.cnt = 0
        self.seen = {}


class K:
    def __init__(self, nc):
        self.nc = nc
        self.E = {
            "pe": Eng(nc, "pe", nc.tensor),
            "act": Eng(nc, "act", nc.scalar),
            "dve": Eng(nc, "dve", nc.vector),
            "pool": Eng(nc, "pool", nc.gpsimd),
            "sp": Eng(nc, "sp", nc.sync),
        }
        self.dsem = {q: [[nc.alloc_semaphore("ds_%s%d" % (q, i)), 0] for i in range(NDS)] for q in ("sp", "pool", "act")}
        self.dk = {"sp": 0, "pool": 0, "act": 0}
        self.conv_q = []
        self.conv_i = 0
        self.pump_every = 1
        self.op_count = 0
        self.arena = nc.alloc_sbuf_tensor("arena", [128, ARENA], U8).ap()
        self.off = 0
        self.live = []
        self.retired = []
        ps0 = nc.alloc_psum_tensor("psA", [128, 2048], F32).ap()
        ps1 = nc.alloc_psum_tensor("psB", [128, 2048], F32).ap()
        self.P = [Tile(ps0[:, 512 * i:512 * (i + 1)], Buf("P%d" % i)) for i in range(4)] + \
                 [Tile(ps1[:, 512 * i:512 * (i + 1)], Buf("P%d" % (4 + i))) for i in range(4)]
        self.n_ins = 0

    def _waits(self, E, e, r, w):
        need = {}
        for b in r:
            for sem, v in b.w.items():
                if sem is E.sem and (e == "pe" or v > E.cnt):
                    continue
                if v > need.get(sem, 0):
                    need[sem] = v
        for b in w:
            for dd in (b.w, b.r):
                for sem, v in dd.items():
                    if sem is E.sem:
                        continue
                    if v > need.get(sem, 0):
                        need[sem] = v
        for sem, v in need.items():
            if E.seen.get(sem, 0) >= v:
                continue
            E.h.wait_ge(sem, v)
            E.seen[sem] = v

    def pump(self, n=1):
        while n > 0 and self.conv_i < len(self.conv_q):
            fn = self.conv_q[self.conv_i]
            self.conv_i += 1
            fn()
            n -= 1

    def pump_until(self, idx):
        while self.conv_i < min(idx, len(self.conv_q)):
            self.pump(1)

    def op(self, e, fn, r=(), w=(), inc=True):
        self.op_count += 1
        if self.op_count % self.pump_every == 0:
            self.pump(1)
        E = self.E[e]
        self._waits(E, e, r, w)
        ins = fn()
        self.n_ins += 1
        if inc:
            E.cnt += 1
            ins.then_inc(E.sem, 1)
            tok = E.cnt
        else:
            tok = E.cnt + 1
        for b in r:
            if b.r.get(E.sem, 0) < tok:
                b.r[E.sem] = tok
        for b in w:
            if b.w.get(E.sem, 0) < tok:
                b.w[E.sem] = tok
        return ins

    def dma(self, q, out, in_, r=(), w=()):
        E = self.E[q]
        self._waits(E, q, r, w)
        k = self.dk[q]
        self.dk[q] = (k + 1) % NDS
        sem, total = self.dsem[q][k]
        if total > 0 and E.seen.get(sem, 0) < total:
            E.h.wait_ge(sem, total)
            E.seen[sem] = total
        ins = E.h.dma_start(out=out, in_=in_, allow_slow_non_contiguous=True)
        ins.then_inc(sem, 16)
        total += 16
        self.dsem[q][k][1] = total
        self.n_ins += 1
        for b in r:
            b.r[sem] = total
        for b in w:
            b.w[sem] = total

    def finish(self, bufs):
        E = self.E["sp"]
        for b in bufs:
            for sem, v in b.w.items():
                if E.seen.get(sem, 0) < v:
                    E.h.wait_ge(sem, v)
                    E.seen[sem] = v
        for q in self.dsem:
            for sem, total in self.dsem[q]:
                if total > 0 and E.seen.get(sem, 0) < total:
                    E.h.wait_ge(sem, total)
                    E.seen[sem] = total

    def mark(self):
        return self.off

    def release(self, m):
        self.off = m

    def tile(self, shape, dt, name="", at=None):
        esz = 2 if dt == BF16 else (4 if dt in (F32, I32) else 1)
        n = 1
        for s in shape[1:]:
            n *= s
        nbytes = (n * esz + 63) // 64 * 64
        if at is None:
            off = self.off
            self.off += nbytes
        else:
            off = at
        assert off + nbytes <= ARENA, "arena overflow %s %d" % (name, off + nbytes)
        ap = self.arena[:, off:off + n * esz]
        if dt != U8:
            ap = ap.bitcast(dt)
        if len(shape) == 3:
            ap = ap.rearrange("p (a b) -> p a b", a=shape[1])
        elif len(shape) == 4:
            ap = ap.rearrange("p (a b c) -> p a b c", a=shape[1], b=shape[2])
        b = Buf(name)
        for (o, s, rb) in self.live:
            if o < off + nbytes and off < o + s:
                for sem, v in rb.w.items():
                    if v > b.w.get(sem, 0):
                        b.w[sem] = v
                for sem, v in rb.r.items():
                    if v > b.r.get(sem, 0):
                        b.r[sem] = v
        self.live.append((off, nbytes, b))
        return Tile(ap, b)


def build_program(dbg=None, stop_after=None):
    nc = bass.Bass("TRN2", target_bir_lowering=False)
    k = K(nc)
    dt_in = {}

    def din(name, shape):
        t = nc.dram_tensor(name, list(shape), F32, kind="ExternalInput").ap()
        dt_in[name] = t
        return t

    x_in = din("x", [L, D])
    consts = din("consts", [128, C_TOT])
    ln_mix_even = din("ln_mix_even", [1, D])
    w_in = din("w_in", [1, D, 3072])
    conv_w = din("conv_w", [1, 31, 1024])
    conv_b = din("conv_b", [1, 1024])
    conv_ln_g = din("conv_ln_g", [1, 1024])
    conv_ln_b = din("conv_ln_b", [1, 1024])
    lam_re = din("ssm_lam_re", [1, 64, 64])
    lam_im = din("ssm_lam_im", [1, 64, 64])
    log_step = din("ssm_log_step", [1, 64])
    b_re = din("ssm_b_re", [1, 64, 64, 16])
    b_im = din("ssm_b_im", [1, 64, 64, 16])
    c_re = din("ssm_c_re", [1, 64, 16, 64])
    c_im = din("ssm_c_im", [1, 64, 16, 64])
    ssm_d = din("ssm_d", [1, 64, 16])
    w_glu = din("ssm_w_glu", [1, 1024, 1024])
    b_glu = din("ssm_b_glu", [1, 1024])
    w_out_even = din("w_out_even", [1, D, D])
    ln_mix_odd = din("ln_mix_odd", [1, D])
    w_qkv = din("w_qkv", [1, D, 3 * D])
    w_o = din("w_o", [1, D, D])
    ln_ffn = din("ln_ffn", [2, D])
    ffn_w_up = din("ffn_w_up", [2, D, 2 * DFF])
    ffn_conv_w = din("ffn_conv_w", [2, 3, 2 * DFF])
    ffn_w_down = din("ffn_w_down", [2, DFF, D])
    ln_final = din("ln_final", [D])
    out = nc.dram_tensor("out", [L, D], F32, kind="ExternalOutput").ap()
    out_b = Buf("out")

    def scratch(name, shape, dt=F32):
        kind = "ExternalOutput" if (dbg and name in dbg) else "Internal"
        return Tile(nc.dram_tensor(name, list(shape), dt, kind=kind).ap(), Buf(name))

    xT = [scratch("xT%d" % i, [D, L]) for i in range(5)]
    oT_d = scratch("oT", [D, L], BF16)
    accD = scratch("accD", [1024, L])
    dbg_t = {}
    if dbg:
        for nm in ("hT0", "yaT", "ybT", "usT"):
            if nm in dbg:
                dbg_t[nm] = scratch(nm, [2048 if nm == "hT0" else 1024, L], BF16)

    nobuf = Buf("const_in")

    cf = k.tile([128, 256 + 8 + 2048], F32, "cf")
    cb = k.tile([128, 256 + 2048 + 128], BF16, "cb")
    k.dma("sp", cf.ap[:, 0:256], consts[:, C_ID:C_ID + 256], r=[nobuf], w=[cf.b])
    k.dma("sp", cf.ap[:, 256:264 + 2048], consts[:, C_RM:C_RM + 8 + 2048], r=[nobuf], w=[cf.b])
    k.dma("pool", cb.ap[:, 0:2304], consts[:, C_TRI:C_TRI + 256 + 2048], r=[nobuf], w=[cb.b])
    k.dma("pool", cb.ap[:, 2304:2432], consts[:, C_ID:C_ID + 128], r=[nobuf], w=[cb.b])
    ident = cf.ap[:, 0:128]
    ones_f = cf.ap[:, 128:256]
    RM = cf.ap[:, 256:264]
    tvec = cf.ap[:, 264:264 + 2048]
    tri_neg = cb.ap[:, 0:128]
    ones_neg = cb.ap[:, 128:256]
    masks = [cb.ap[:, 256 + 512 * i:256 + 512 * (i + 1)] for i in range(4)]
    ident_bf = cb.ap[:, 2304:2432]
    P = k.P

    def load_vec_cols(dst_tile, col0, vec_ap, nchunk):
        k.dma("sp", dst_tile.ap[:, col0:col0 + nchunk], vec_ap.rearrange("(c p) -> p c", p=128),
              r=[nobuf], w=[dst_tile.b])

    class WC:
        def __init__(self, name, wd, kc_n, slabs, width):
            self.kc_n = kc_n
            self.width = width
            self.n = len(slabs)
            self.dram = nc.dram_tensor(name, [len(slabs), 128, kc_n * width], BF16).ap()
            self.bufs = [Buf(name + "_%d" % i) for i in range(len(slabs))]
            self.conv_end = []
            for si, runs in enumerate(slabs):
                off = 0
                for (c0, ncols) in runs:
                    for k0 in range(0, kc_n, 4):
                        k1 = min(kc_n, k0 + 4)
                        self._reg(wd, si, k0, k1, off, c0, ncols)
                    off += ncols
                assert off == width
                self.conv_end.append(len(k.conv_q))

        def _reg(self, wd, si, k0, k1, off, c0, ncols):
            dst = self.dram[si].rearrange("p (kc n) -> p kc n", kc=self.kc_n)[:, k0:k1, off:off + ncols]
            src = wd[k0 * 128:k1 * 128, c0:c0 + ncols].rearrange("(kc p) n -> p kc n", p=128)
            buf = self.bufs[si]
            k.conv_q.append(lambda: k.dma("pool", dst, src, r=[nobuf], w=[buf]))

    def load_cached(slab, wc, si):
        k.pump_until(wc.conv_end[si])
        k.dma("sp", slab.ap.rearrange("p a b -> p (a b)"), wc.dram[si], r=[wc.bufs[si]], w=[slab.b])

    def rms_norm(src, gcol, gc0, dst_fn, t0, ntiles, ps_tile, tmp):
        nx = [0]
        for ti in range(ntiles):
            tok = t0 + ti * TW
            for c in range(16):
                xc = tmp["xc"][nx[0] % 3]
                nx[0] += 1
                k.dma("sp", xc.ap, src.ap[c * 128:(c + 1) * 128, tok:tok + TW], r=[src.b], w=[xc.b])
                sq = tmp["sq"][c % 2]
                k.op("act", lambda: nc.scalar.activation(out=sq.ap, in_=xc.ap, func=AF.Square), r=[xc.b], w=[sq.b])
                k.op("pe", lambda: nc.tensor.matmul(ps_tile.ap, lhsT=ones_f, rhs=sq.ap, start=(c == 0), stop=(c == 15)),
                     r=[sq.b, cf.b], w=[ps_tile.b])
            rstd = tmp["rstd"]
            k.op("act", lambda: nc.scalar.activation(out=rstd.ap, in_=ps_tile.ap, func=AF.Sqrt, bias=EPS, scale=1.0 / D),
                 r=[ps_tile.b], w=[rstd.b])
            k.op("dve", lambda: nc.vector.reciprocal(out=rstd.ap, in_=rstd.ap), r=[rstd.b], w=[rstd.b])
            for c in range(16):
                xc = tmp["xc"][nx[0] % 3]
                nx[0] += 1
                k.dma("sp", xc.ap, src.ap[c * 128:(c + 1) * 128, tok:tok + TW], r=[src.b], w=[xc.b])
                dap, dbuf = dst_fn(c, ti)
                k.op("dve", lambda: nc.vector.scalar_tensor_tensor(out=dap, in0=xc.ap, scalar=gcol.ap[:, gc0 + c:gc0 + c + 1],
                                                                  in1=rstd.ap, op0=ALU.mult, op1=ALU.mult),
                     r=[xc.b, gcol.b, rstd.b], w=[dbuf])

    def norm_tmp(pfx):
        return {"xc": [k.tile([128, TW], F32, pfx + "xc%d" % i) for i in range(3)],
                "sq": [k.tile([128, TW], F32, pfx + "sq%d" % i) for i in range(2)],
                "rstd": k.tile([128, TW], F32, pfx + "rstd")}

    def linear_fm(wc, slab_ids, rhs_fn, epilogue, banks, ntiles=NT, slab_tiles=None, preloaded=False):
        bi = 0
        kc_n = wc.kc_n
        nbuf = len(slab_tiles)
        if not preloaded:
            load_cached(slab_tiles[0], wc, slab_ids[0])
        for sp_, si in enumerate(slab_ids):
            slab = slab_tiles[sp_ % nbuf]
            if nbuf >= 2 and sp_ + 1 < len(slab_ids):
                load_cached(slab_tiles[(sp_ + 1) % nbuf], wc, slab_ids[sp_ + 1])
            for mi in range(wc.width // 128):
                for ti in range(ntiles):
                    ps = banks[bi % len(banks)]
                    bi += 1
                    for kc in range(kc_n):
                        rap, rbuf = rhs_fn(kc, ti)
                        k.op("pe", lambda: nc.tensor.matmul(ps.ap, lhsT=slab.ap[:, kc, mi * 128:(mi + 1) * 128], rhs=rap,
                                                            start=(kc == 0), stop=(kc == kc_n - 1)),
                             r=[slab.b, rbuf], w=[ps.b], inc=(kc == kc_n - 1))
                    epilogue(sp_, mi, ti, ps)

    def residual_epilogue(src, dst, tmp_x, tmp_o, chunk_fn, t0=0):
        cnt = [0]

        def ep(si, mi, ti, ps):
            c = chunk_fn(si, mi)
            tok = t0 + ti * TW
            xr = tmp_x[cnt[0] % len(tmp_x)]
            ot = tmp_o[cnt[0] % len(tmp_o)]
            cnt[0] += 1
            k.dma("sp", xr.ap, src.ap[c * 128:(c + 1) * 128, tok:tok + TW], r=[src.b], w=[xr.b])
            k.op("dve", lambda: nc.vector.tensor_tensor(out=ot.ap, in0=ps.ap, in1=xr.ap, op=ALU.add),
                 r=[ps.b, xr.b], w=[ot.b])
            k.dma("sp", dst.ap[c * 128:(c + 1) * 128, tok:tok + TW], ot.ap, r=[ot.b], w=[dst.b])
        return ep

    def dump(tile_ap, buf, dram_tile, r0, c0, ncols):
        k.dma("sp", dram_tile.ap[r0:r0 + 128, c0:c0 + ncols], tile_ap, r=[buf], w=[dram_tile.b])

    wc_in_conv = WC("wc_in_conv", w_in[0], 16, [[(256 * i, 256), (1024 + 256 * i, 256)] for i in range(4)], 512)
    wc_in_ssm = WC("wc_in_ssm", w_in[0], 16, [[(2048 + 512 * i, 512)] for i in range(2)], 512)
    wc_glu = WC("wc_glu", w_glu[0], 8, [[(512 * i, 512)] for i in range(2)], 512)
    wc_out = WC("wc_out", w_out_even[0], 16, [[(512 * i, 512)] for i in range(4)], 512)
    wc_up = [None, None]
    wc_dn = [None, None]

    def mk_ffn_wc(l):
        wc_up[l] = WC("wc_up%d" % l, ffn_w_up[l], 16, [[(256 * i, 256), (DFF + 256 * i, 256)] for i in range(NPAIR // 2)], 512)
        wc_dn[l] = WC("wc_dn%d" % l, ffn_w_down[l], NPAIR, [[(256 * i, 256)] for i in range(8)], 256)
    mk_ffn_wc(0)
    qkv_slabs = []
    for hg in range(4):
        qkv_slabs += [[(512 * hg, 512)], [(2048 + 512 * hg, 512)], [(4096 + 512 * hg, 512)]]
    wc_qkv = WC("wc_qkv", w_qkv[0], 16, qkv_slabs, 512)
    wc_o = WC("wc_o", w_o[0], 16, [[(512 * i, 512)] for i in range(4)], 512)
    mk_ffn_wc(1)

    def phase_transpose_in():
        m = k.mark()
        xin = [k.tile([128, D], F32, "xin%d" % i) for i in range(4)]
        st = [k.tile([128, TW], F32, "tst%d" % i) for i in range(3)]
        n = 0
        for tg in range(NT):
            for s in range(4):
                r0 = tg * TW + s * 128
                k.dma("sp", xin[s].ap, x_in[r0:r0 + 128, :], r=[nobuf], w=[xin[s].b])
            for c in range(16):
                ps = P[n % 4]
                stt = st[n % 3]
                n += 1
                for s in range(4):
                    k.op("pe", lambda: nc.tensor.transpose(out=ps.ap[:, s * 128:(s + 1) * 128], in_=xin[s].ap[:, c * 128:(c + 1) * 128],
                                                           identity=ident),
                         r=[xin[s].b, cf.b], w=[ps.b], inc=(s == 3))
                k.op("act", lambda: nc.scalar.copy(out=stt.ap, in_=ps.ap), r=[ps.b], w=[stt.b])
                k.dma("sp", xT[0].ap[c * 128:(c + 1) * 128, tg * TW:(tg + 1) * TW], stt.ap, r=[stt.b], w=[xT[0].b])
        k.release(m)

    def phase_mixer0():
        KB = 1024
        base_off = k.off
        assert base_off <= 14 * KB
        W = w_in[0]
        gcols = k.tile([128, 64], F32, "gcols", at=14 * KB)
        cw = k.tile([128, 8, 31], F32, "cw", at=14 * KB + 512)
        tab = k.tile([128, 8, 32], F32, "s5tab", at=15 * KB + 512)
        stt = k.tile([128, 32, 2], F32, "s5state", at=16 * KB + 512)
        DD = k.tile([128, 8, 128], BF16, "DD", at=17 * KB)
        hT = k.tile([128, 16, L], BF16, "hT", at=19 * KB)
        slabs_t = [k.tile([128, 16, 512], BF16, "slab%d" % i, at=(83 + 16 * i) * KB) for i in range(2)]
        yaT = k.tile([128, 8, L], BF16, "yaT", at=115 * KB)
        load_vec_cols(gcols, 0, ln_mix_even[0], 16)
        load_vec_cols(gcols, 16, conv_b[0], 8)
        load_vec_cols(gcols, 24, conv_ln_g[0], 8)
        load_vec_cols(gcols, 32, conv_ln_b[0], 8)
        load_vec_cols(gcols, 40, ssm_d[0].rearrange("g h -> (g h)"), 8)
        load_vec_cols(gcols, 48, b_glu[0], 8)
        for c in range(8):
            k.dma("sp", cw.ap[:, c, :], conv_w[0][:, c * 128:(c + 1) * 128].rearrange("k p -> p k"), r=[nobuf], w=[cw.b])
        k.off = 147 * KB
        tmp = norm_tmp("")
        rms_norm(xT[0], gcols, 0, lambda c, ti: (hT.ap[:, c, ti * TW:(ti + 1) * TW], hT.b), 0, NT, P[7], tmp)
        if "hT0" in dbg_t:
            for c in range(16):
                dump(hT.ap[:, c, :], hT.b, dbg_t["hT0"], c * 128, 0, L)
        k.off = 147 * KB
        accc = [k.tile([128, L], F32, "accc%d" % i) for i in range(2)]
        hpad = [k.tile([128, L + 32], BF16, "hpad%d" % i) for i in range(2)]
        sig = [k.tile([128, TW], F32, "sig%d" % i) for i in range(2)]
        dgs = [k.tile([128, 31, 128], BF16, "dg%d" % i) for i in range(2)]
        for hp in hpad:
            k.op("dve", lambda: nc.vector.memset(hp.ap[:, 0:32], 0.0), w=[hp.b])
        bi = 0
        cpend = []
        ncv = [0]
        load_cached(slabs_t[0], wc_in_conv, 0)
        for si in range(4):
            slab = slabs_t[si % 2]
            if si + 1 < 4:
                load_cached(slabs_t[(si + 1) % 2], wc_in_conv, si + 1)
            for lc in range(2):
                c = 2 * si + lc
                hp = hpad[c % 2]
                acc = accc[c % 2]
                dg = dgs[c % 2]
                for kk in range(31):
                    k.op("dve", lambda: nc.vector.tensor_scalar(out=dg.ap[:, kk, :], in0=ident, scalar1=cw.ap[:, c, kk:kk + 1], scalar2=None,
                                                               op0=ALU.mult), r=[cf.b, cw.b], w=[dg.b])
                for ti in range(NT):
                    psv = P[bi % 6]
                    psg = P[(bi + 1) % 6]
                    bi += 2
                    for (ps, mi) in ((psg, 2 + lc), (psv, lc)):
                        for kc in range(16):
                            k.op("pe", lambda: nc.tensor.matmul(ps.ap, lhsT=slab.ap[:, kc, mi * 128:(mi + 1) * 128],
                                                                rhs=hT.ap[:, kc, ti * TW:(ti + 1) * TW], start=(kc == 0), stop=(kc == 15)),
                                 r=[slab.b, hT.b], w=[ps.b], inc=(kc == 15))
                    if ti == 1:
                        for fn_ in cpend:
                            fn_()
                        del cpend[:]
                    sg = sig[ti % 2]
                    k.op("act", lambda: nc.scalar.activation(out=sg.ap, in_=psg.ap, func=AF.Sigmoid), r=[psg.b], w=[sg.b])
                    k.op("dve", lambda: nc.vector.tensor_tensor(out=hp.ap[:, 32 + ti * TW:32 + (ti + 1) * TW], in0=psv.ap, in1=sg.ap, op=ALU.mult),
                         r=[psv.b, sg.b], w=[hp.b])

                def conv_mm(c=c, hp=hp, acc=acc, dg=dg):
                    for ti in range(NT):
                        ps = P[6 + ncv[0] % 2]
                        ncv[0] += 1
                        for kk in range(31):
                            k.op("pe", lambda: nc.tensor.matmul(ps.ap, lhsT=dg.ap[:, kk, :], rhs=hp.ap[:, 2 + kk + ti * TW:2 + kk + (ti + 1) * TW],
                                                                start=(kk == 0), stop=(kk == 30)),
                                 r=[dg.b, hp.b], w=[ps.b], inc=(kk == 30))
                        k.op("act", lambda: nc.scalar.activation(out=acc.ap[:, ti * TW:(ti + 1) * TW], in_=ps.ap, func=AF.Identity,
                                                                 bias=gcols.ap[:, 16 + c:17 + c], scale=1.0),
                             r=[ps.b, gcols.b], w=[acc.b])
                    k.dma("sp", accD.ap[c * 128:(c + 1) * 128, :], acc.ap, r=[acc.b], w=[accD.b])
                cpend.append(conv_mm)
        for fn_ in cpend:
            fn_()
        del cpend[:]
        k.off = 147 * KB
        accr = k.tile([128, 8, TW], F32, "accr")
        sq = [k.tile([128, TW], F32, "lsq%d" % i) for i in range(2)]
        mean = k.tile([128, TW], F32, "mean")
        rstd = k.tile([128, TW], F32, "lrstd")
        tmpc = [k.tile([128, TW], F32, "ltmp%d" % i) for i in range(2)]
        for ti in range(NT):
            sl = slice(ti * TW, (ti + 1) * TW)
            k.dma("sp", accr.ap, accD.ap[:, sl].rearrange("(c p) t -> p c t", p=128), r=[accD.b], w=[accr.b])
            for c in range(8):
                k.op("pe", lambda: nc.tensor.matmul(P[0].ap, lhsT=ones_f, rhs=accr.ap[:, c, :], start=(c == 0), stop=(c == 7)),
                     r=[accr.b, cf.b], w=[P[0].b], inc=(c == 7))
            for c in range(8):
                s2 = sq[c % 2]
                k.op("act", lambda: nc.scalar.activation(out=s2.ap, in_=accr.ap[:, c, :], func=AF.Square), r=[accr.b], w=[s2.b])
                k.op("pe", lambda: nc.tensor.matmul(P[1].ap, lhsT=ones_f, rhs=s2.ap, start=(c == 0), stop=(c == 7)),
                     r=[s2.b, cf.b], w=[P[1].b])
            k.op("act", lambda: nc.scalar.mul(out=mean.ap, in_=P[0].ap, mul=1.0 / 1024), r=[P[0].b], w=[mean.b])
            t0_ = tmpc[0]
            k.op("dve", lambda: nc.vector.tensor_tensor(out=t0_.ap, in0=mean.ap, in1=mean.ap, op=ALU.mult), r=[mean.b], w=[t0_.b])
            k.op("dve", lambda: nc.vector.scalar_tensor_tensor(out=rstd.ap, in0=P[1].ap, scalar=1.0 / 1024, in1=t0_.ap,
                                                              op0=ALU.mult, op1=ALU.subtract), r=[P[1].b, t0_.b], w=[rstd.b])
            k.op("act", lambda: nc.scalar.activation(out=rstd.ap, in_=rstd.ap, func=AF.Sqrt, bias=EPS, scale=1.0), r=[rstd.b], w=[rstd.b])
            k.op("dve", lambda: nc.vector.reciprocal(out=rstd.ap, in_=rstd.ap), r=[rstd.b], w=[rstd.b])
            for c in range(8):
                tt = tmpc[c % 2]
                k.op("dve", lambda: nc.vector.tensor_tensor(out=tt.ap, in0=accr.ap[:, c, :], in1=mean.ap, op=ALU.subtract),
                     r=[accr.b, mean.b], w=[tt.b])
                k.op("dve", lambda: nc.vector.tensor_tensor(out=tt.ap, in0=tt.ap, in1=rstd.ap, op=ALU.mult), r=[tt.b, rstd.b], w=[tt.b])
                k.op("act", lambda: nc.scalar.activation(out=yaT.ap[:, c, sl], in_=tt.ap, func=AF.Silu,
                                                         bias=gcols.ap[:, 32 + c:33 + c], scale=gcols.ap[:, 24 + c:25 + c]),
                     r=[tt.b, gcols.b], w=[yaT.b])
        if "yaT" in dbg_t:
            for c in range(8):
                dump(yaT.ap[:, c, :], yaT.b, dbg_t["yaT"], c * 128, 0, L)
        if stop_after == "conv":
            return
        usT = k.tile([128, 8, L], BF16, "usT", at=147 * KB)

        def us_ep(si, mi, ti, ps):
            c = 4 * si + mi
            k.op("act", lambda: nc.scalar.copy(out=usT.ap[:, c, ti * TW:(ti + 1) * TW], in_=ps.ap), r=[ps.b], w=[usT.b])
        linear_fm(wc_in_ssm, [0, 1], lambda kc, ti: (hT.ap[:, kc, ti * TW:(ti + 1) * TW], hT.b), us_ep, P[0:4], slab_tiles=slabs_t)
        if "usT" in dbg_t:
            for c in range(8):
                dump(usT.ap[:, c, :], usT.b, dbg_t["usT"], c * 128, 0, L)
        ygT = k.tile([128, 8, L], BF16, "ygT", at=19 * KB)
        LB = k.tile([128, 32, 2, 128], BF16, "LB", at=51 * KB)
        LC = k.tile([128, 32, 2, 128], BF16, "LC", at=67 * KB)
        k.off = 179 * KB
        lr, li, stp, th, mg, fre, fim, tq = [tab.ap[:, i, :] for i in range(8)]
        tb = tab.b
        for two in range(2):
            ps_ = slice(64 * two, 64 * two + 64)
            k.dma("sp", tab.ap[ps_, 0, :], lam_re[0][two::2, :].rearrange("j p -> p j"), r=[nobuf], w=[tb])
            k.dma("sp", tab.ap[ps_, 1, :], lam_im[0][two::2, :].rearrange("j p -> p j"), r=[nobuf], w=[tb])
            k.dma("sp", tab.ap[ps_, 2, :], log_step[0][two::2].unsqueeze(0).to_broadcast([64, 32]), r=[nobuf], w=[tb])
        t2 = k.tile([128, 8, 32], F32, "s5tab2")
        cs, sn, nre, nim, den, u1, u2, u3 = [t2.ap[:, i, :] for i in range(8)]
        t2b = t2.b
        PI = math.pi

        def dv(fn, r, w):
            k.op("dve", fn, r=r, w=w)

        def ac(fn, r, w):
            k.op("act", fn, r=r, w=w)
        ac(lambda: nc.scalar.activation(out=stp, in_=stp, func=AF.Exp), [tb], [tb])
        dv(lambda: nc.vector.tensor_tensor(out=th, in0=li, in1=stp, op=ALU.mult), [tb], [tb])
        dv(lambda: nc.vector.tensor_tensor(out=tq, in0=lr, in1=stp, op=ALU.mult), [tb], [tb])
        ac(lambda: nc.scalar.activation(out=mg, in_=tq, func=AF.Exp), [tb], [tb])
        dv(lambda: nc.vector.tensor_scalar(out=th, in0=th, scalar1=1.0 / (2 * PI), scalar2=None, op0=ALU.mult), [tb], [tb])
        nis = k.tile([128, 32], I32, "s5ni")
        dv(lambda: nc.vector.tensor_copy(out=nis.ap, in_=th), [tb], [nis.b])
        dv(lambda: nc.vector.tensor_tensor(out=u1, in0=th, in1=nis.ap, op=ALU.subtract), [tb, nis.b], [t2b])
        dv(lambda: nc.vector.scalar_tensor_tensor(out=u2, in0=u1, scalar=-1.0, in1=u1, op0=ALU.mult, op1=ALU.max), [t2b], [t2b])
        ac(lambda: nc.scalar.activation(out=sn, in_=u1, func=AF.Sin, scale=TWO_PI_S), [t2b], [t2b])
        ac(lambda: nc.scalar.activation(out=cs, in_=u2, func=AF.Sin, scale=-TWO_PI_S, bias=0.5 * PI), [t2b], [t2b])
        dv(lambda: nc.vector.tensor_tensor(out=nre, in0=mg, in1=cs, op=ALU.mult), [tb, t2b], [t2b])
        dv(lambda: nc.vector.tensor_scalar(out=nre, in0=nre, scalar1=-1.0, scalar2=None, op0=ALU.add), [t2b], [t2b])
        dv(lambda: nc.vector.tensor_tensor(out=nim, in0=mg, in1=sn, op=ALU.mult), [tb, t2b], [t2b])
        dv(lambda: nc.vector.tensor_tensor(out=den, in0=lr, in1=lr, op=ALU.mult), [tb], [t2b])
        dv(lambda: nc.vector.tensor_tensor(out=u3, in0=li, in1=li, op=ALU.mult), [tb], [t2b])
        dv(lambda: nc.vector.tensor_tensor(out=den, in0=den, in1=u3, op=ALU.add), [t2b], [t2b])
        dv(lambda: nc.vector.reciprocal(out=den, in_=den), [t2b], [t2b])
        dv(lambda: nc.vector.tensor_tensor(out=u1, in0=nre, in1=lr, op=ALU.mult), [tb, t2b], [t2b])
        dv(lambda: nc.vector.tensor_tensor(out=u2, in0=nim, in1=li, op=ALU.mult), [tb, t2b], [t2b])
        dv(lambda: nc.vector.tensor_tensor(out=u1, in0=u1, in1=u2, op=ALU.add), [t2b], [t2b])
        dv(lambda: nc.vector.tensor_tensor(out=fre, in0=u1, in1=den, op=ALU.mult), [t2b], [tb])
        dv(lambda: nc.vector.tensor_tensor(out=u1, in0=nim, in1=lr, op=ALU.mult), [tb, t2b], [t2b])
        dv(lambda: nc.vector.tensor_tensor(out=u2, in0=nre, in1=li, op=ALU.mult), [tb, t2b], [t2b])
        dv(lambda: nc.vector.tensor_tensor(out=u1, in0=u1, in1=u2, op=ALU.subtract), [t2b], [t2b])
        dv(lambda: nc.vector.tensor_tensor(out=fim, in0=u1, in1=den, op=ALU.mult), [t2b], [tb])
        braw = [k.tile([128, 32, 16], F32, "braw%d" % i) for i in range(2)]
        for ri, bsrc in enumerate((b_re, b_im)):
            for two in range(2):
                k.dma("sp", braw[ri].ap[64 * two:64 * two + 64], bsrc[0][two::2].rearrange("j p h -> p j h"),
                      r=[nobuf], w=[braw[ri].b])
        bb = [k.tile([128, 32, 16], F32, "bb%d" % i) for i in range(2)]
        btmp = k.tile([128, 32, 16], F32, "btmp")
        fre_b = tab.ap[:, 5, :].unsqueeze(2).to_broadcast([128, 32, 16])
        fim_b = tab.ap[:, 6, :].unsqueeze(2).to_broadcast([128, 32, 16])
        dv(lambda: nc.vector.tensor_tensor(out=bb[0].ap, in0=braw[0].ap, in1=fre_b, op=ALU.mult), [braw[0].b, tb], [bb[0].b])
        dv(lambda: nc.vector.tensor_tensor(out=btmp.ap, in0=braw[1].ap, in1=fim_b, op=ALU.mult), [braw[1].b, tb], [btmp.b])
        dv(lambda: nc.vector.tensor_tensor(out=bb[0].ap, in0=bb[0].ap, in1=btmp.ap, op=ALU.subtract), [bb[0].b, btmp.b], [bb[0].b])
        dv(lambda: nc.vector.tensor_tensor(out=bb[1].ap, in0=braw[1].ap, in1=fre_b, op=ALU.mult), [braw[1].b, tb], [bb[1].b])
        dv(lambda: nc.vector.tensor_tensor(out=btmp.ap, in0=braw[0].ap, in1=fim_b, op=ALU.mult), [braw[0].b, tb], [btmp.b])
        dv(lambda: nc.vector.tensor_tensor(out=bb[1].ap, in0=bb[1].ap, in1=btmp.ap, op=ALU.add), [bb[1].b, btmp.b], [bb[1].b])
        pad = k.tile([128, 32, 2, 128], F32, "s5pad", at=83 * KB)
        dv(lambda: nc.vector.memset(pad.ap, 0.0), [], [pad.b])
        for ri in range(2):
            for two in range(2):
                for jm in range(4):
                    col = 32 * jm + 16 * two
                    o_ap = pad.ap[64 * two:64 * two + 64, :, ri, col:col + 16].rearrange("p (c m) h -> p c m h", m=4)[:, :, jm, :]
                    i_ap = bb[ri].ap[64 * two:64 * two + 64].rearrange("p (c m) h -> p c m h", m=4)[:, :, jm, :]
                    dv(lambda: nc.vector.tensor_copy(out=o_ap, in_=i_ap), [bb[ri].b], [pad.b])

        def transpose_pad(dst):
            n = 0
            for j in range(0, 32, 2):
                ps = P[n % 4]
                n += 1
                for jj in range(2):
                    for ri in range(2):
                        q = jj * 2 + ri
                        k.op("pe", lambda: nc.tensor.transpose(out=ps.ap[:, q * 128:(q + 1) * 128], in_=pad.ap[:, j + jj, ri, :], identity=ident),
                             r=[pad.b, cf.b], w=[ps.b], inc=(q == 3))
                k.op("act", lambda: nc.scalar.copy(out=dst.ap[:, j:j + 2].rearrange("p a b c -> p (a b c)"), in_=ps.ap), r=[ps.b], w=[dst.b])
        transpose_pad(LB)
        craw = [k.tile([128, 8, 64], F32, "craw%d" % i) for i in range(2)]
        for ri, csrc in enumerate((c_re, c_im)):
            k.dma("sp", craw[ri].ap, csrc[0].rearrange("(c gm) ho p -> (gm ho) c p", gm=8), r=[nobuf], w=[craw[ri].b])
        for ri in range(2):
            for two in range(2):
                for jm in range(4):
                    o_ap = pad.ap[:, :, ri, 64 * two:64 * two + 64].rearrange("p (c m) s -> p c m s", m=4)[:, :, jm, :]
                    g8 = 2 * jm + two
                    dv(lambda: nc.vector.tensor_scalar(out=o_ap, in0=craw[ri].ap, scalar1=RM[:, g8:g8 + 1], scalar2=(-1.0 if ri else 1.0),
                                                       op0=ALU.mult, op1=ALU.mult), [craw[ri].b, cf.b], [pad.b])
        transpose_pad(LC)
        for c in range(8):
            dv(lambda: nc.vector.tensor_scalar(out=DD.ap[:, c, :], in0=ident, scalar1=gcols.ap[:, 40 + c:41 + c], scalar2=None, op0=ALU.mult),
               [cf.b, gcols.b], [DD.b])
        dv(lambda: nc.vector.memset(stt.ap, 0.0), [], [stt.b])
        om = k.tile([128, 4, 32], F32, "s5om", at=179 * KB)
        nio = k.tile([128, 32], I32, "s5nio", at=179 * KB + 2048)
        dv(lambda: nc.vector.tensor_scalar(out=om.ap[:, 2, :], in0=tab.ap[:, 3, :], scalar1=512.0, scalar2=None, op0=ALU.mult), [tb], [om.b])
        dv(lambda: nc.vector.tensor_copy(out=nio.ap, in_=om.ap[:, 2, :]), [om.b], [nio.b])
        dv(lambda: nc.vector.tensor_tensor(out=om.ap[:, 2, :], in0=om.ap[:, 2, :], in1=nio.ap, op=ALU.subtract), [om.b, nio.b], [om.b])
        dv(lambda: nc.vector.scalar_tensor_tensor(out=om.ap[:, 3, :], in0=om.ap[:, 2, :], scalar=-1.0, in1=om.ap[:, 2, :],
                                                  op0=ALU.mult, op1=ALU.max), [om.b], [om.b])
        ac(lambda: nc.scalar.activation(out=om.ap[:, 1, :], in_=om.ap[:, 2, :], func=AF.Sin, scale=TWO_PI_S), [om.b], [om.b])
        ac(lambda: nc.scalar.activation(out=om.ap[:, 0, :], in_=om.ap[:, 3, :], func=AF.Sin, scale=-TWO_PI_S, bias=0.5 * PI), [om.b], [om.b])
        sti = k.tile([128, 4, 2], F32, "s5sti", at=179 * KB + 2304)
        rtm = k.tile([128, 4, 4], F32, "s5rtm", at=179 * KB + 2368)
        k.off = 83 * KB
        ncs4 = [k.tile([128, 2, TW], BF16, "ncs%d" % i) for i in range(4)]
        bub = [k.tile([128, 2, TW], BF16, "bub%d" % i) for i in range(2)]
        xt_ = [k.tile([128, 2, TW], BF16, "xt%d" % i) for i in range(2)]
        sc_ = [k.tile([128, 2, TW], BF16, "sc%d" % i) for i in range(2)]
        tmD = k.tile([128, 2, TW], BF16, "tmD")
        tmP = k.tile([128, 2, TW], BF16, "tmP")
        sb_ = [k.tile([128, 2, TW], BF16, "sb%d" % i) for i in range(2)]
        k.off = 182 * KB
        ph = k.tile([128, 2, TW], F32, "ph")
        nit = k.tile([128, TW], I32, "nit")
        ysb = k.tile([128, TW], F32, "ysb")
        yt = [k.tile([128, TW], F32, "yt%d" % i) for i in range(2)]
        it = 0
        GC = 2.0 * math.sqrt(2.0 / math.pi)

        def pl(fn, r, w):
            k.op("pool", fn, r=r, w=w)
        pend = []
        for c in range(8):
            for jm in range(4):
                j = 4 * c + jm
                t_ = ncs4[jm]
                dv(lambda: nc.vector.tensor_scalar(out=ph.ap[:, 1, :], in0=tvec[:, 0:TW], scalar1=tab.ap[:, 3, j:j + 1], scalar2=None,
                                                   op0=ALU.mult), [cf.b, tb], [ph.b])
                dv(lambda: nc.vector.tensor_copy(out=nit.ap, in_=ph.ap[:, 1, :]), [ph.b], [nit.b])
                dv(lambda: nc.vector.tensor_tensor(out=ph.ap[:, 1, :], in0=ph.ap[:, 1, :], in1=nit.ap, op=ALU.subtract),
                   [ph.b, nit.b], [ph.b])
                dv(lambda: nc.vector.scalar_tensor_tensor(out=ph.ap[:, 0, :], in0=ph.ap[:, 1, :], scalar=-1.0, in1=ph.ap[:, 1, :],
                                                          op0=ALU.mult, op1=ALU.max), [ph.b], [ph.b])
                ac(lambda: nc.scalar.activation(out=t_.ap[:, 1, :], in_=ph.ap[:, 1, :], func=AF.Sin, scale=TWO_PI_S), [ph.b], [t_.b])
                ac(lambda: nc.scalar.activation(out=t_.ap[:, 0, :], in_=ph.ap[:, 0, :], func=AF.Sin, scale=-TWO_PI_S, bias=0.5 * PI),
                   [ph.b], [t_.b])
            dv(lambda: nc.vector.memset(sti.ap, 0.0), [], [sti.b])
            for qd in range(NT):
                sl = slice(qd * TW, (qd + 1) * TW)
                if qd > 0:
                    sr_, si2 = stt.ap[:, 4 * c:4 * c + 4, 0], stt.ap[:, 4 * c:4 * c + 4, 1]
                    oc, os_ = om.ap[:, 0, 4 * c:4 * c + 4], om.ap[:, 1, 4 * c:4 * c + 4]
                    dv(lambda: nc.vector.tensor_tensor(out=rtm.ap[:, 0, :], in0=sr_, in1=oc, op=ALU.mult), [stt.b, om.b], [rtm.b])
                    dv(lambda: nc.vector.tensor_tensor(out=rtm.ap[:, 1, :], in0=si2, in1=os_, op=ALU.mult), [stt.b, om.b], [rtm.b])
                    dv(lambda: nc.vector.tensor_tensor(out=rtm.ap[:, 2, :], in0=sr_, in1=os_, op=ALU.mult), [stt.b, om.b], [rtm.b])
                    dv(lambda: nc.vector.tensor_tensor(out=rtm.ap[:, 3, :], in0=si2, in1=oc, op=ALU.mult), [stt.b, om.b], [rtm.b])
                    dv(lambda: nc.vector.tensor_tensor(out=sti.ap[:, :, 0], in0=rtm.ap[:, 0, :], in1=rtm.ap[:, 1, :], op=ALU.subtract),
                       [rtm.b], [sti.b])
                    dv(lambda: nc.vector.tensor_tensor(out=sti.ap[:, :, 1], in0=rtm.ap[:, 2, :], in1=rtm.ap[:, 3, :], op=ALU.add),
                       [rtm.b], [sti.b])
                yps = P[6 + (c * NT + qd) % 2]
                k.op("pe", lambda: nc.tensor.matmul(yps.ap, lhsT=DD.ap[:, c, :], rhs=usT.ap[:, c, sl], start=True, stop=False),
                     r=[DD.b, usT.b], w=[yps.b], inc=False)
                for jm in range(4):
                    j = 4 * c + jm
                    u = it % 2
                    it += 1
                    pr, pi_ = P[(it % 3) * 2], P[(it % 3) * 2 + 1]
                    for ri, pp in ((0, pr), (1, pi_)):
                        k.op("pe", lambda: nc.tensor.matmul(pp.ap, lhsT=LB.ap[:, j, ri, :], rhs=usT.ap[:, c, sl], start=True, stop=True),
                             r=[LB.b, usT.b], w=[pp.b])
                    for fn_ in pend:
                        fn_()
                    del pend[:]
                    cs_t = ncs4[jm]
                    ncos, nsin = cs_t.ap[:, 0, :], cs_t.ap[:, 1, :]
                    ac(lambda: nc.scalar.copy(out=bub[u].ap[:, 0, :], in_=pr.ap), [pr.b], [bub[u].b])
                    ac(lambda: nc.scalar.copy(out=bub[u].ap[:, 1, :], in_=pi_.ap), [pi_.b], [bub[u].b])
                    br_, bi_ = bub[u].ap[:, 0, :], bub[u].ap[:, 1, :]
                    dv(lambda: nc.vector.tensor_tensor(out=xt_[u].ap[:, 0, :], in0=br_, in1=ncos, op=ALU.mult), [bub[u].b, cs_t.b], [xt_[u].b])
                    dv(lambda: nc.vector.tensor_tensor(out=tmD.ap[:, 0, :], in0=bi_, in1=nsin, op=ALU.mult), [bub[u].b, cs_t.b], [tmD.b])
                    dv(lambda: nc.vector.tensor_tensor(out=xt_[u].ap[:, 1, :], in0=bi_, in1=ncos, op=ALU.mult), [bub[u].b, cs_t.b], [xt_[u].b])
                    dv(lambda: nc.vector.tensor_tensor(out=tmD.ap[:, 1, :], in0=br_, in1=nsin, op=ALU.mult), [bub[u].b, cs_t.b], [tmD.b])
                    dv(lambda: nc.vector.tensor_tensor(out=xt_[u].ap[:, 0, :], in0=xt_[u].ap[:, 0, :], in1=tmD.ap[:, 0, :], op=ALU.add),
                       [xt_[u].b, tmD.b], [xt_[u].b])
                    dv(lambda: nc.vector.tensor_tensor(out=xt_[u].ap[:, 1, :], in0=xt_[u].ap[:, 1, :], in1=tmD.ap[:, 1, :], op=ALU.subtract),
                       [xt_[u].b, tmD.b], [xt_[u].b])
                    a_b = tab.ap[:, 4, j:j + 1].to_broadcast([128, TW])
                    for ri in range(2):
                        dv(lambda: nc.vector.tensor_tensor_scan(out=sc_[u].ap[:, ri, :], data0=a_b, data1=xt_[u].ap[:, ri, :],
                                                                initial=sti.ap[:, jm, ri:ri + 1], op0=ALU.mult, op1=ALU.add),
                           [tb, xt_[u].b, sti.b], [sc_[u].b])
                    dv(lambda: nc.vector.tensor_copy(out=stt.ap[:, j, :], in_=sc_[u].ap[:, :, TW - 1]), [sc_[u].b], [stt.b])
                    sr, si_ = sc_[u].ap[:, 0, :], sc_[u].ap[:, 1, :]
                    dv(lambda: nc.vector.tensor_tensor(out=tmP.ap[:, 0, :], in0=sr, in1=ncos, op=ALU.mult), [sc_[u].b, cs_t.b], [tmP.b])
                    dv(lambda: nc.vector.tensor_tensor(out=tmP.ap[:, 1, :], in0=si_, in1=nsin, op=ALU.mult), [sc_[u].b, cs_t.b], [tmP.b])
                    dv(lambda: nc.vector.tensor_tensor(out=sb_[u].ap[:, 0, :], in0=tmP.ap[:, 0, :], in1=tmP.ap[:, 1, :], op=ALU.subtract),
                       [tmP.b], [sb_[u].b])
                    dv(lambda: nc.vector.tensor_tensor(out=tmP.ap[:, 0, :], in0=sr, in1=nsin, op=ALU.mult), [sc_[u].b, cs_t.b], [tmP.b])
                    dv(lambda: nc.vector.tensor_tensor(out=tmP.ap[:, 1, :], in0=si_, in1=ncos, op=ALU.mult), [sc_[u].b, cs_t.b], [tmP.b])
                    dv(lambda: nc.vector.tensor_tensor(out=sb_[u].ap[:, 1, :], in0=tmP.ap[:, 0, :], in1=tmP.ap[:, 1, :], op=ALU.add),
                       [tmP.b], [sb_[u].b])

                    def ymm(j=j, u=u, jm=jm, yps=yps):
                        for ri in range(2):
                            last = (jm == 3 and ri == 1)
                            k.op("pe", lambda: nc.tensor.matmul(yps.ap, lhsT=LC.ap[:, j, ri, :], rhs=sb_[u].ap[:, ri, :], start=False, stop=last),
                                 r=[LC.b, sb_[u].b], w=[yps.b], inc=True)
                    pend.append(ymm)

                def gelu_ep(c=c, sl=sl, yps=yps):
                    ac(lambda: nc.scalar.copy(out=ysb.ap, in_=yps.ap), [yps.b], [ysb.b])
                    dv(lambda: nc.vector.tensor_tensor(out=yt[0].ap, in0=ysb.ap, in1=ysb.ap, op=ALU.mult), [ysb.b], [yt[0].b])
                    dv(lambda: nc.vector.tensor_scalar(out=yt[0].ap, in0=yt[0].ap, scalar1=0.044715, scalar2=1.0, op0=ALU.mult, op1=ALU.add),
                       [yt[0].b], [yt[0].b])
                    dv(lambda: nc.vector.tensor_tensor(out=yt[0].ap, in0=yt[0].ap, in1=ysb.ap, op=ALU.mult), [yt[0].b, ysb.b], [yt[0].b])
                    ac(lambda: nc.scalar.activation(out=yt[1].ap, in_=yt[0].ap, func=AF.Sigmoid, scale=GC), [yt[0].b], [yt[1].b])
                    dv(lambda: nc.vector.tensor_tensor(out=ygT.ap[:, c, sl], in0=ysb.ap, in1=yt[1].ap, op=ALU.mult), [ysb.b, yt[1].b], [ygT.b])
                pend.append(gelu_ep)
        for fn_ in pend:
            fn_()
        del pend[:]
        k.off = 83 * KB
        ybT = usT
        gsl = [k.tile([128, 8, 512], BF16, "gslab%d" % i) for i in range(2)]
        sg2 = [k.tile([128, TW], F32, "sg2_%d" % i) for i in range(2)]
        cnt = [0]

        def glu_ep(si, mi, ti, ps):
            c = 4 * si + mi
            s = sg2[cnt[0] % 2]
            cnt[0] += 1
            sl = slice(ti * TW, (ti + 1) * TW)
            ac(lambda: nc.scalar.activation(out=s.ap, in_=ps.ap, func=AF.Sigmoid, bias=gcols.ap[:, 48 + c:49 + c], scale=1.0),
               [ps.b, gcols.b], [s.b])
            dv(lambda: nc.vector.tensor_tensor(out=ybT.ap[:, c, sl], in0=ygT.ap[:, c, sl], in1=s.ap, op=ALU.mult), [ygT.b, s.b], [ybT.b])
        linear_fm(wc_glu, [0, 1], lambda kc, ti: (ygT.ap[:, kc, ti * TW:(ti + 1) * TW], ygT.b), glu_ep, P[0:4], slab_tiles=gsl)
        if "ybT" in dbg_t:
            for c in range(8):
                dump(ybT.ap[:, c, :], ybT.b, dbg_t["ybT"], c * 128, 0, L)
        slabs_t = [k.tile([128, 16, 512], BF16, "oslab%d" % i, at=(83 + 16 * i) * KB) for i in range(2)]
        k.off = 179 * KB
        tx = [k.tile([128, TW], F32, "tx%d" % i) for i in range(3)]
        to = [k.tile([128, TW], F32, "to%d" % i) for i in range(3)]

        def cat_rhs(kc, ti):
            sl = slice(ti * TW, (ti + 1) * TW)
            if kc < 8:
                return yaT.ap[:, kc, sl], yaT.b
            return ybT.ap[:, kc - 8, sl], ybT.b
        linear_fm(wc_out, [0, 1, 2, 3], cat_rhs,
                  residual_epilogue(xT[0], xT[1], tx, to, lambda si, mi: 4 * si + mi), P[4:8], slab_tiles=slabs_t)
        k.off = base_off

    def phase_ffn(layer, src, dst):
        m0 = k.mark()
        wu = wc_up[layer]
        wdn = wc_dn[layer]
        gcol = k.tile([128, 16], F32, "fgcol")
        load_vec_cols(gcol, 0, ln_ffn[layer], 16)
        fcw = k.tile([128, 88, 3], F32, "fcw")
        for kk in range(3):
            for c0 in range(0, 88, 22):
                k.dma("sp", fcw.ap[:, c0:c0 + 22, kk], ffn_conv_w[layer][kk, c0 * 128:(c0 + 22) * 128].rearrange("(c p) -> p c", p=128),
                      r=[nobuf], w=[fcw.b])
        carry = k.tile([128, 88, 2], F32, "carry")
        k.op("dve", lambda: nc.vector.memset(carry.ap, 0.0), w=[carry.b])
        hbs = [k.tile([128, 16, TW], BF16, "hb%d" % i) for i in range(2)]
        gT = k.tile([128, NPAIR, TW], BF16, "gT")
        NUB = 2
        uslab = [k.tile([128, 16, 512], BF16, "uslab%d" % i) for i in range(NUB)]
        dslab = [k.tile([128, NPAIR, 256], BF16, "dslab%d" % i) for i in range(2)]
        tmp = norm_tmp("f")
        upad = [k.tile([128, TW + 2], F32, "upad%d" % i) for i in range(4)]
        accg = [k.tile([128, TW], F32, "accg%d" % i) for i in range(2)]
        accv = [k.tile([128, TW], F32, "accv%d" % i) for i in range(2)]
        tx = [k.tile([128, TW], F32, "ftx%d" % i) for i in range(2)]
        to = [k.tile([128, TW], F32, "fto%d" % i) for i in range(2)]
        nup = 0
        nb = 0
        nsl = 0
        load_cached(uslab[0], wu, 0)
        rms_norm(src, gcol, 0, lambda c, ti: (hbs[0].ap[:, c, :], hbs[0].b), 0, 1, P[5], tmp)
        for blk in range(NT):
            t0 = blk * TW
            hb = hbs[blk % 2]
            NS2 = NPAIR // 2
            for s2 in range(NS2):
                slab = uslab[nsl % NUB]
                nxt = s2 + NUB - 1
                if nxt < NS2:
                    load_cached(uslab[(nsl + NUB - 1) % NUB], wu, nxt)
                elif blk + 1 < NT:
                    load_cached(uslab[(nsl + NUB - 1) % NUB], wu, nxt - NS2)
                nsl += 1
                if s2 == 8 and blk + 1 < NT:
                    hn = hbs[(blk + 1) % 2]
                    rms_norm(src, gcol, 0, lambda c, ti: (hn.ap[:, c, :], hn.b), t0 + TW, 1, P[5], tmp)
                if s2 == 17:
                    load_cached(dslab[0], wdn, 0)
                for lp in range(2):
                    pair = 2 * s2 + lp
                    fg, fv = pair, NPAIR + pair
                    psg, psv = P[nb % 5], P[(nb + 1) % 5]
                    nb += 2
                    upg, upv = upad[nup % 4], upad[(nup + 1) % 4]
                    nup += 2
                    ag, av = accg[pair % 2], accv[pair % 2]
                    for (ps, mi) in ((psg, lp), (psv, 2 + lp)):
                        for kc in range(16):
                            k.op("pe", lambda: nc.tensor.matmul(ps.ap, lhsT=slab.ap[:, kc, mi * 128:(mi + 1) * 128], rhs=hb.ap[:, kc, :],
                                                                start=(kc == 0), stop=(kc == 15)),
                                 r=[slab.b, hb.b], w=[ps.b], inc=(kc == 15))
                    for (ps, up, f) in ((psg, upg, fg), (psv, upv, fv)):
                        k.op("act", lambda: nc.scalar.copy(out=up.ap[:, 0:2], in_=carry.ap[:, f, :]), r=[carry.b], w=[up.b])
                        k.op("act", lambda: nc.scalar.copy(out=up.ap[:, 2:TW + 2], in_=ps.ap), r=[ps.b], w=[up.b])
                    for (up, f) in ((upg, fg), (upv, fv)):
                        k.op("act", lambda: nc.scalar.copy(out=carry.ap[:, f, :], in_=up.ap[:, TW:TW + 2]), r=[up.b], w=[carry.b])
                    for tap in (2, 1, 0):
                        for (up, f, a) in ((upg, fg, ag), (upv, fv, av)):
                            if tap == 2:
                                k.op("dve", lambda: nc.vector.tensor_scalar(out=a.ap, in0=up.ap[:, 2:TW + 2], scalar1=fcw.ap[:, f, 2:3],
                                                                           scalar2=None, op0=ALU.mult), r=[up.b, fcw.b], w=[a.b])
                            else:
                                k.op("dve", lambda: nc.vector.scalar_tensor_tensor(out=a.ap, in0=up.ap[:, tap:TW + tap],
                                                                                  scalar=fcw.ap[:, f, tap:tap + 1], in1=a.ap,
                                                                                  op0=ALU.mult, op1=ALU.add), r=[up.b, fcw.b, a.b], w=[a.b])
                    k.op("act", lambda: nc.scalar.activation(out=ag.ap, in_=ag.ap, func=AF.Silu), r=[ag.b], w=[ag.b])
                    k.op("dve", lambda: nc.vector.tensor_tensor(out=gT.ap[:, pair, :], in0=ag.ap, in1=av.ap, op=ALU.mult),
                         r=[ag.b, av.b], w=[gT.b])
            linear_fm(wdn, list(range(8)), lambda kc, ti: (gT.ap[:, kc, :], gT.b),
                      residual_epilogue(src, dst, tx, to, lambda si, mi: 2 * si + mi, t0=t0), [P[6], P[7]], ntiles=1, slab_tiles=dslab,
                      preloaded=True)
        k.release(m0)

    def phase_attn(src, dst):
        m0 = k.mark()
        Wq = w_qkv[0]
        gcol = k.tile([128, 16], F32, "agcol")
        load_vec_cols(gcol, 0, ln_mix_odd[0], 16)
        hT = k.tile([128, 16, L], BF16, "ahT")
        m1 = k.mark()
        tmp = norm_tmp("a")
        rms_norm(src, gcol, 0, lambda c, ti: (hT.ap[:, c, ti * TW:(ti + 1) * TW], hT.b), 0, NT, P[7], tmp)
        k.release(m1)
        slabs_t = [k.tile([128, 16, 512], BF16, "aslab%d" % i) for i in range(2)]
        qT = k.tile([128, 4, L], BF16, "qT")
        kT = k.tile([128, 4, L], BF16, "kT")
        vS = k.tile([128, 16, 512], BF16, "vS")
        oh = [k.tile([128, L], BF16, "oh%d" % i) for i in range(2)]
        NB3 = 3
        e_t = [k.tile([128, TW], F32, "e_t%d" % i) for i in range(NB3)]
        sp_t = [k.tile([128, TW], BF16, "sp_t%d" % i) for i in range(NB3)]
        arg_t = [k.tile([128, TW], F32, "arg_t%d" % i) for i in range(NB3)]
        w_t = [k.tile([128, TW], BF16, "w_t%d" % i) for i in range(NB3)]
        Rts = [k.tile([128, TW], F32, "Rt%d" % i) for i in range(2)]
        scale = 128.0 ** -0.5
        hrhs = lambda kc, ti: (hT.ap[:, kc, ti * TW:(ti + 1) * TW], hT.b)
        nsl = 0
        for hg in range(4):
            def q_ep(si, mi, ti, ps):
                k.op("act", lambda: nc.scalar.mul(out=qT.ap[:, mi, ti * TW:(ti + 1) * TW], in_=ps.ap, mul=scale), r=[ps.b], w=[qT.b])

            def k_ep(si, mi, ti, ps):
                k.op("dve", lambda: nc.vector.tensor_copy(out=kT.ap[:, mi, ti * TW:(ti + 1) * TW], in_=ps.ap), r=[ps.b], w=[kT.b])
            if hg == 0:
                load_cached(slabs_t[0], wc_qkv, 0)
            load_cached(slabs_t[(nsl + 1) % 2], wc_qkv, 3 * hg + 1)
            linear_fm(wc_qkv, [3 * hg], hrhs, q_ep, P[0:4], slab_tiles=[slabs_t[nsl % 2]], preloaded=True)
            nsl += 1
            load_cached(slabs_t[(nsl + 1) % 2], wc_qkv, 3 * hg + 2)
            linear_fm(wc_qkv, [3 * hg + 1], hrhs, k_ep, P[0:4], slab_tiles=[slabs_t[nsl % 2]], preloaded=True)
            nsl += 1
            vsl = slabs_t[nsl % 2]
            nsl += 1
            for tt in range(16):
                ps = P[tt % 4]
                for kc in range(16):
                    k.op("pe", lambda: nc.tensor.matmul(ps.ap, lhsT=hT.ap[:, kc, tt * 128:(tt + 1) * 128], rhs=vsl.ap[:, kc, :],
                                                        start=(kc == 0), stop=(kc == 15)), r=[hT.b, vsl.b], w=[ps.b], inc=(kc == 15))
                k.op("act", lambda: nc.scalar.copy(out=vS.ap[:, tt, :], in_=ps.ap), r=[ps.b], w=[vS.b])
            if hg + 1 < 4:
                load_cached(slabs_t[nsl % 2], wc_qkv, 3 * (hg + 1))
            tiles = []
            seqi = 0
            for hl in range(4):
                for J in range(NT):
                    nblk = 4 * J + 4
                    for b in range(nblk - 1, -1, -1):
                        tiles.append((hl, J, b, nblk, seqi))
                    seqi += 1
            NTL = len(tiles)

            def stA(n):
                hl, J, b, nblk, sq_ = tiles[n]
                zps = P[n % 2]
                dg = b >= 4 * J
                k.op("pe", lambda: nc.tensor.matmul(zps.ap, lhsT=kT.ap[:, hl, b * 128:(b + 1) * 128], rhs=qT.ap[:, hl, J * TW:(J + 1) * TW],
                                                    start=True, stop=not dg), r=[kT.b, qT.b], w=[zps.b], inc=not dg)
                if dg:
                    k.op("pe", lambda: nc.tensor.matmul(zps.ap, lhsT=ident_bf, rhs=masks[b - 4 * J], start=False, stop=True),
                         r=[cb.b], w=[zps.b])

            def stB(n):
                hl, J, b, nblk, sq_ = tiles[n]
                zps = P[n % 2]
                u = n % NB3
                k.op("act", lambda: nc.scalar.activation(out=e_t[u].ap, in_=zps.ap, func=AF.Exp), r=[zps.b], w=[e_t[u].b])
                k.op("act", lambda: nc.scalar.activation(out=sp_t[u].ap, in_=e_t[u].ap, func=AF.Ln, bias=1.0, scale=1.0),
                     r=[e_t[u].b], w=[sp_t[u].b])

            def stC(n):
                hl, J, b, nblk, sq_ = tiles[n]
                u = n % NB3
                aps, cps = P[2 + n % 2], P[4 + n % 2]
                k.op("pe", lambda: nc.tensor.matmul(aps.ap, lhsT=tri_neg, rhs=sp_t[u].ap, start=True, stop=False),
                     r=[cb.b, sp_t[u].b], w=[aps.b], inc=False)
                dg = b >= 4 * J
                k.op("pe", lambda: nc.tensor.matmul(aps.ap, lhsT=kT.ap[:, hl, b * 128:(b + 1) * 128], rhs=qT.ap[:, hl, J * TW:(J + 1) * TW],
                                                    start=False, stop=not dg), r=[kT.b, qT.b], w=[aps.b], inc=not dg)
                if dg:
                    k.op("pe", lambda: nc.tensor.matmul(aps.ap, lhsT=ident_bf, rhs=masks[b - 4 * J], start=False, stop=True),
                         r=[cb.b], w=[aps.b])
                k.op("pe", lambda: nc.tensor.matmul(cps.ap, lhsT=ones_neg, rhs=sp_t[u].ap, start=True, stop=True),
                     r=[cb.b, sp_t[u].b], w=[cps.b])

            def stD(n):
                hl, J, b, nblk, sq_ = tiles[n]
                u = n % NB3
                aps, cps = P[2 + n % 2], P[4 + n % 2]
                Rt = Rts[sq_ % 2]
                if b == nblk - 1:
                    k.op("act", lambda: nc.scalar.activation(out=w_t[u].ap, in_=aps.ap, func=AF.Exp), r=[aps.b], w=[w_t[u].b])
                    k.op("dve", lambda: nc.vector.tensor_copy(out=Rt.ap, in_=cps.ap), r=[cps.b], w=[Rt.b])
                else:
                    k.op("dve", lambda: nc.vector.tensor_tensor(out=arg_t[u].ap, in0=aps.ap, in1=Rt.ap, op=ALU.add),
                         r=[aps.b, Rt.b], w=[arg_t[u].b])
                    if b > 0:
                        k.op("dve", lambda: nc.vector.tensor_tensor(out=Rt.ap, in0=cps.ap, in1=Rt.ap, op=ALU.add),
                             r=[cps.b, Rt.b], w=[Rt.b])
                    k.op("act", lambda: nc.scalar.activation(out=w_t[u].ap, in_=arg_t[u].ap, func=AF.Exp), r=[arg_t[u].b], w=[w_t[u].b])

            def stE(n):
                hl, J, b, nblk, sq_ = tiles[n]
                u = n % NB3
                ops_ = P[6 + sq_ % 2]
                k.op("pe", lambda: nc.tensor.matmul(ops_.ap, lhsT=vS.ap[:, b, hl * 128:(hl + 1) * 128], rhs=w_t[u].ap,
                                                    start=(b == nblk - 1), stop=(b == 0)), r=[vS.b, w_t[u].b], w=[ops_.b], inc=True)
                if b == 0:
                    h = 4 * hg + hl
                    ohh = oh[h % 2]
                    k.op("act", lambda: nc.scalar.copy(out=ohh.ap[:, J * TW:(J + 1) * TW], in_=ops_.ap), r=[ops_.b], w=[ohh.b])
                    if J == NT - 1:
                        k.dma("sp", oT_d.ap[h * 128:(h + 1) * 128, :], ohh.ap, r=[ohh.b], w=[oT_d.b])

            for n in range(NTL + 2):
                if n < NTL:
                    stA(n)
                    stB(n)
                if 0 <= n - 1 < NTL:
                    stC(n - 1)
                    stD(n - 1)
                if 0 <= n - 2 < NTL:
                    stE(n - 2)
        k.release(m0)
        m2 = k.mark()
        oT = k.tile([128, 16, L], BF16, "oT")
        for c in range(16):
            k.dma("sp", oT.ap[:, c, :], oT_d.ap[c * 128:(c + 1) * 128, :], r=[oT_d.b], w=[oT.b])
        slabs_o = [k.tile([128, 16, 512], BF16, "oslab%d" % i) for i in range(2)]
        tx = [k.tile([128, TW], F32, "otx%d" % i) for i in range(3)]
        to = [k.tile([128, TW], F32, "oto%d" % i) for i in range(3)]
        linear_fm(wc_o, [0, 1, 2, 3], lambda kc, ti: (oT.ap[:, kc, ti * TW:(ti + 1) * TW], oT.b),
                  residual_epilogue(src, dst, tx, to, lambda si, mi: 4 * si + mi), P[0:4], slab_tiles=slabs_o)
        k.release(m2)

    def phase_final(src):
        m0 = k.mark()
        gcol = k.tile([128, 16], F32, "zgcol")
        load_vec_cols(gcol, 0, ln_final, 16)
        tmp = norm_tmp("z")
        yT = k.tile([128, 16, TW], F32, "zyT")
        st = [k.tile([128, TW], F32, "zst%d" % i) for i in range(3)]
        n = 0
        for ti in range(NT):
            rms_norm(src, gcol, 0, lambda c, t_: (yT.ap[:, c, :], yT.b), ti * TW, 1, P[7], tmp)
            for s in range(4):
                for cg in range(4):
                    ps = P[n % 4]
                    stt = st[n % 3]
                    n += 1
                    for ci in range(4):
                        c = 4 * cg + ci
                        k.op("pe", lambda: nc.tensor.transpose(out=ps.ap[:, ci * 128:(ci + 1) * 128], in_=yT.ap[:, c, s * 128:(s + 1) * 128],
                                                               identity=ident), r=[yT.b, cf.b], w=[ps.b], inc=(ci == 3))
                    k.op("act", lambda: nc.scalar.copy(out=stt.ap, in_=ps.ap), r=[ps.b], w=[stt.b])
                    r0 = ti * TW + s * 128
                    k.dma("sp", out[r0:r0 + 128, cg * 512:(cg + 1) * 512], stt.ap, r=[stt.b], w=[out_b])
        k.release(m0)

    phases = [
        ("tin", phase_transpose_in),
        ("mix0", phase_mixer0),
        ("ffn0", lambda: phase_ffn(0, xT[1], xT[2])),
        ("attn", lambda: phase_attn(xT[2], xT[3])),
        ("ffn1", lambda: phase_ffn(1, xT[3], xT[4])),
        ("final", lambda: phase_final(xT[4])),
    ]
    run = build_program.run_phases
    for name, fn in phases:
        if run is None or name in run:
            fn()
    fin = [out_b] + [t.b for t in xT] + [oT_d.b] + [t.b for t in dbg_t.values()]
    k.finish(fin)
    return nc, k


build_program.run_phases = None

WEIGHT_KEYS = ["ln_mix_even", "w_in", "conv_w", "conv_b", "conv_ln_g", "conv_ln_b", "ssm_lam_re", "ssm_lam_im", "ssm_log_step",
               "ssm_b_re", "ssm_b_im", "ssm_c_re", "ssm_c_im", "ssm_d", "ssm_w_glu", "ssm_b_glu", "w_out_even", "ln_mix_odd",
               "w_qkv", "w_o", "ln_ffn", "ffn_w_up", "ffn_conv_w", "ffn_w_down", "ln_final"]


def kernel(**inputs):
    nc, _ = build_program()
    x = np.ascontiguousarray(np.asarray(inputs["x"], dtype=np.float32))
    consts = make_consts()
    shared = {kk: np.ascontiguousarray(np.asarray(inputs[kk], dtype=np.float32)) for kk in WEIGHT_KEYS}
    in_maps = []
    for b in range(8):
        m = dict(shared)
        m["x"] = x[b]
        m["consts"] = consts
        in_maps.append(m)
    res = run_bass_kernel_spmd(nc, in_maps, core_ids=list(range(8)))
    return np.stack([np.asarray(r["out"], dtype=np.float32) for r in res.results], axis=0)
```

```python
import math
import numpy as np
import concourse.bass as bass
import concourse.mybir as mybir
from concourse.bass_utils import run_bass_kernel_spmd

F32 = mybir.dt.float32
BF16 = mybir.dt.bfloat16
U8 = mybir.dt.uint8
I32 = mybir.dt.int32
TWO_PI_S = 6.28318
AF = mybir.ActivationFunctionType
ALU = mybir.AluOpType

L = 2048
D = 2048
TW = 512
NT = L // TW
DFF = 5632
NPAIR = DFF // 128
EPS = 1e-6
ARENA = 205 * 1024
NDS = 24

C_ID, C_ONES, C_TRI, C_ONEG, C_MASK, C_RM, C_TV = 0, 128, 256, 384, 512, 2560, 2568
C_TOT = 2568 + 2048


def make_consts():
    c = np.zeros((128, C_TOT), np.float32)
    c[:, C_ID:C_ID + 128] = np.eye(128, dtype=np.float32)
    c[:, C_ONES:C_ONES + 128] = 1.0
    j = np.arange(128)[:, None]
    s = np.arange(128)[None, :]
    c[:, C_TRI:C_TRI + 128] = -(j >= s).astype(np.float32)
    c[:, C_ONEG:C_ONEG + 128] = -1.0
    cc = np.arange(512)[None, :]
    for i in range(4):
        c[:, C_MASK + 512 * i:C_MASK + 512 * (i + 1)] = np.where((128 * i + j) < cc, 0.0, -100.0).astype(np.float32)
    c[:, C_RM:C_RM + 8] = ((j // 16) == np.arange(8)[None, :]).astype(np.float32)
    c[:, C_TV:C_TV + 2048] = np.arange(2048, dtype=np.float32)[None, :]
    return c


class Buf:
    __slots__ = ("w", "r", "name")

    def __init__(self, name=""):
        self.w = {}
        self.r = {}
        self.name = name


class Tile:
    __slots__ = ("ap", "b")

    def __init__(self, ap, b):
        self.ap = ap
        self.b = b


class Eng:
    def __init__(self, nc, name, h):
        self.name = name
        self.h = h
        self.sem = nc.alloc_semaphore("es_" + name)
        self.cnt = 0
        self.seen = {}


class K:
    def __init__(self, nc):
        self.nc = nc
        self.E = {
            "pe": Eng(nc, "pe", nc.tensor),
            "act": Eng(nc, "act", nc.scalar),
            "dve": Eng(nc, "dve", nc.vector),
            "pool": Eng(nc, "pool", nc.gpsimd),
            "sp": Eng(nc, "sp", nc.sync),
        }
        self.dsem = {q: [[nc.alloc_semaphore("ds_%s%d" % (q, i)), 0] for i in range(NDS)] for q in ("sp", "pool", "act")}
        self.dk = {"sp": 0, "pool": 0, "act": 0}
        self.conv_q = []
        self.conv_i = 0
        self.pump_every = 1
        self.pump_limit = None
        self.op_count = 0
        self.arena = nc.alloc_sbuf_tensor("arena", [128, ARENA], U8).ap()
        self.off = 0
        self.live = []
        self.retired = []
        ps0 = nc.alloc_psum_tensor("psA", [128, 2048], F32).ap()
        ps1 = nc.alloc_psum_tensor("psB", [128, 2048], F32).ap()
        self.P = [Tile(ps0[:, 512 * i:512 * (i + 1)], Buf("P%d" % i)) for i in range(4)] + \
                 [Tile(ps1[:, 512 * i:512 * (i + 1)], Buf("P%d" % (4 + i))) for i in range(4)]
        self.n_ins = 0

    def _waits(self, E, e, r, w):
        need = {}
        for b in r:
            for sem, v in b.w.items():
                if sem is E.sem and (e == "pe" or v > E.cnt):
                    continue
                if v > need.get(sem, 0):
                    need[sem] = v
        for b in w:
            for dd in (b.w, b.r):
                for sem, v in dd.items():
                    if sem is E.sem:
                        continue
                    if v > need.get(sem, 0):
                        need[sem] = v
        for sem, v in need.items():
            if E.seen.get(sem, 0) >= v:
                continue
            E.h.wait_ge(sem, v)
            E.seen[sem] = v

    def pump(self, n=1):
        lim = len(self.conv_q) if self.pump_limit is None else min(self.pump_limit, len(self.conv_q))
        while n > 0 and self.conv_i < lim:
            fn = self.conv_q[self.conv_i]
            self.conv_i += 1
            fn()
            n -= 1

    def pump_until(self, idx):
        while self.conv_i < min(idx, len(self.conv_q)):
            self.pump(1)

    def op(self, e, fn, r=(), w=(), inc=True):
        self.op_count += 1
        if self.op_count % self.pump_every == 0:
            self.pump(1)
        E = self.E[e]
        self._waits(E, e, r, w)
        ins = fn()
        self.n_ins += 1
        if inc:
            E.cnt += 1
            ins.then_inc(E.sem, 1)
            tok = E.cnt
        else:
            tok = E.cnt + 1
        for b in r:
            if b.r.get(E.sem, 0) < tok:
                b.r[E.sem] = tok
        for b in w:
            if b.w.get(E.sem, 0) < tok:
                b.w[E.sem] = tok
        return ins

    def dma(self, q, out, in_, r=(), w=()):
        E = self.E[q]
        self._waits(E, q, r, w)
        k = self.dk[q]
        self.dk[q] = (k + 1) % NDS
        sem, total = self.dsem[q][k]
        if total > 0 and E.seen.get(sem, 0) < total:
            E.h.wait_ge(sem, total)
            E.seen[sem] = total
        ins = E.h.dma_start(out=out, in_=in_, allow_slow_non_contiguous=True)
        ins.then_inc(sem, 16)
        total += 16
        self.dsem[q][k][1] = total
        self.n_ins += 1
        for b in r:
            b.r[sem] = total
        for b in w:
            b.w[sem] = total

    def finish(self, bufs):
        E = self.E["sp"]
        for b in bufs:
            for sem, v in b.w.items():
                if E.seen.get(sem, 0) < v:
                    E.h.wait_ge(sem, v)
                    E.seen[sem] = v
        for q in self.dsem:
            for sem, total in self.dsem[q]:
                if total > 0 and E.seen.get(sem, 0) < total:
                    E.h.wait_ge(sem, total)
                    E.seen[sem] = total

    def mark(self):
        return self.off

    def release(self, m):
        self.off = m

    def tile(self, shape, dt, name="", at=None):
        esz = 2 if dt == BF16 else (4 if dt in (F32, I32) else 1)
        n = 1
        for s in shape[1:]:
            n *= s
        nbytes = (n * esz + 63) // 64 * 64
        if at is None:
            off = self.off
            self.off += nbytes
        else:
            off = at
        assert off + nbytes <= ARENA, "arena overflow %s %d" % (name, off + nbytes)
        ap = self.arena[:, off:off + n * esz]
        if dt != U8:
            ap = ap.bitcast(dt)
        if len(shape) == 3:
            ap = ap.rearrange("p (a b) -> p a b", a=shape[1])
        elif len(shape) == 4:
            ap = ap.rearrange("p (a b c) -> p a b c", a=shape[1], b=shape[2])
        b = Buf(name)
        for (o, s, rb) in self.live:
            if o < off + nbytes and off < o + s:
                for sem, v in rb.w.items():
                    if v > b.w.get(sem, 0):
                        b.w[sem] = v
                for sem, v in rb.r.items():
                    if v > b.r.get(sem, 0):
                        b.r[sem] = v
        self.live.append((off, nbytes, b))
        return Tile(ap, b)


def build_program(dbg=None, stop_after=None):
    nc = bass.Bass("TRN2", target_bir_lowering=False)
    k = K(nc)
    dt_in = {}

    def din(name, shape):
        t = nc.dram_tensor(name, list(shape), F32, kind="ExternalInput").ap()
        dt_in[name] = t
        return t

    x_in = din("x", [L, D])
    consts = din("consts", [128, C_TOT])
    ln_mix_even = din("ln_mix_even", [1, D])
    w_in = din("w_in", [1, D, 3072])
    conv_w = din("conv_w", [1, 31, 1024])
    conv_b = din("conv_b", [1, 1024])
    conv_ln_g = din("conv_ln_g", [1, 1024])
    conv_ln_b = din("conv_ln_b", [1, 1024])
    lam_re = din("ssm_lam_re", [1, 64, 64])
    lam_im = din("ssm_lam_im", [1, 64, 64])
    log_step = din("ssm_log_step", [1, 64])
    b_re = din("ssm_b_re", [1, 64, 64, 16])
    b_im = din("ssm_b_im", [1, 64, 64, 16])
    c_re = din("ssm_c_re", [1, 64, 16, 64])
    c_im = din("ssm_c_im", [1, 64, 16, 64])
    ssm_d = din("ssm_d", [1, 64, 16])
    w_glu = din("ssm_w_glu", [1, 1024, 1024])
    b_glu = din("ssm_b_glu", [1, 1024])
    w_out_even = din("w_out_even", [1, D, D])
    ln_mix_odd = din("ln_mix_odd", [1, D])
    w_qkv = din("w_qkv", [1, D, 3 * D])
    w_o = din("w_o", [1, D, D])
    ln_ffn = din("ln_ffn", [2, D])
    ffn_w_up = din("ffn_w_up", [2, D, 2 * DFF])
    ffn_conv_w = din("ffn_conv_w", [2, 3, 2 * DFF])
    ffn_w_down = din("ffn_w_down", [2, DFF, D])
    ln_final = din("ln_final", [D])
    out = nc.dram_tensor("out", [L, D], F32, kind="ExternalOutput").ap()
    out_b = Buf("out")

    def scratch(name, shape, dt=F32):
        kind = "ExternalOutput" if (dbg and name in dbg) else "Internal"
        return Tile(nc.dram_tensor(name, list(shape), dt, kind=kind).ap(), Buf(name))

    xT = [scratch("xT%d" % i, [D, L]) for i in range(5)]
    oT_d = scratch("oT", [D, L], BF16)
    accD = scratch("accD", [1024, L])
    dbg_t = {}
    if dbg:
        for nm in ("hT0", "yaT", "ybT", "usT"):
            if nm in dbg:
                dbg_t[nm] = scratch(nm, [2048 if nm == "hT0" else 1024, L], BF16)

    nobuf = Buf("const_in")

    cf = k.tile([128, 256 + 8 + 2048], F32, "cf")
    cb = k.tile([128, 256 + 2048 + 128], BF16, "cb")
    k.dma("sp", cf.ap[:, 0:256], consts[:, C_ID:C_ID + 256], r=[nobuf], w=[cf.b])
    k.dma("sp", cf.ap[:, 256:264 + 2048], consts[:, C_RM:C_RM + 8 + 2048], r=[nobuf], w=[cf.b])
    k.dma("pool", cb.ap[:, 0:2304], consts[:, C_TRI:C_TRI + 256 + 2048], r=[nobuf], w=[cb.b])
    k.dma("pool", cb.ap[:, 2304:2432], consts[:, C_ID:C_ID + 128], r=[nobuf], w=[cb.b])
    ident = cf.ap[:, 0:128]
    ones_f = cf.ap[:, 128:256]
    RM = cf.ap[:, 256:264]
    tvec = cf.ap[:, 264:264 + 2048]
    tri_neg = cb.ap[:, 0:128]
    ones_neg = cb.ap[:, 128:256]
    masks = [cb.ap[:, 256 + 512 * i:256 + 512 * (i + 1)] for i in range(4)]
    ident_bf = cb.ap[:, 2304:2432]
    P = k.P

    def load_vec_cols(dst_tile, col0, vec_ap, nchunk):
        k.dma("sp", dst_tile.ap[:, col0:col0 + nchunk], vec_ap.rearrange("(c p) -> p c", p=128),
              r=[nobuf], w=[dst_tile.b])

    class WC:
        def __init__(self, name, wd, kc_n, slabs, width):
            self.kc_n = kc_n
            self.width = width
            self.n = len(slabs)
            self.dram = nc.dram_tensor(name, [len(slabs), 128, kc_n * width], BF16).ap()
            self.bufs = [Buf(name + "_%d" % i) for i in range(len(slabs))]
            self.conv_end = []
            for si, runs in enumerate(slabs):
                off = 0
                for (c0, ncols) in runs:
                    for k0 in range(0, kc_n, 4):
                        k1 = min(kc_n, k0 + 4)
                        self._reg(wd, si, k0, k1, off, c0, ncols)
                    off += ncols
                assert off == width
                self.conv_end.append(len(k.conv_q))

        def _reg(self, wd, si, k0, k1, off, c0, ncols):
            dst = self.dram[si].rearrange("p (kc n) -> p kc n", kc=self.kc_n)[:, k0:k1, off:off + ncols]
            src = wd[k0 * 128:k1 * 128, c0:c0 + ncols].rearrange("(kc p) n -> p kc n", p=128)
            buf = self.bufs[si]
            k.conv_q.append(lambda: k.dma("pool", dst, src, r=[nobuf], w=[buf]))

    def load_cached(slab, wc, si):
        k.pump_until(wc.conv_end[si])
        k.dma("sp", slab.ap.rearrange("p a b -> p (a b)"), wc.dram[si], r=[wc.bufs[si]], w=[slab.b])

    def rms_norm(src, gcol, gc0, dst_fn, t0, ntiles, ps_tile, tmp):
        nx = [0]
        for ti in range(ntiles):
            tok = t0 + ti * TW
            for c in range(16):
                xc = tmp["xc"][nx[0] % 3]
                nx[0] += 1
                k.dma("sp", xc.ap, src.ap[c * 128:(c + 1) * 128, tok:tok + TW], r=[src.b], w=[xc.b])
                sq = tmp["sq"][c % 2]
                k.op("act", lambda: nc.scalar.activation(out=sq.ap, in_=xc.ap, func=AF.Square), r=[xc.b], w=[sq.b])
                k.op("pe", lambda: nc.tensor.matmul(ps_tile.ap, lhsT=ones_f, rhs=sq.ap, start=(c == 0), stop=(c == 15)),
                     r=[sq.b, cf.b], w=[ps_tile.b])
            rstd = tmp["rstd"]
            k.op("act", lambda: nc.scalar.activation(out=rstd.ap, in_=ps_tile.ap, func=AF.Sqrt, bias=EPS, scale=1.0 / D),
                 r=[ps_tile.b], w=[rstd.b])
            k.op("dve", lambda: nc.vector.reciprocal(out=rstd.ap, in_=rstd.ap), r=[rstd.b], w=[rstd.b])
            for c in range(16):
                xc = tmp["xc"][nx[0] % 3]
                nx[0] += 1
                k.dma("sp", xc.ap, src.ap[c * 128:(c + 1) * 128, tok:tok + TW], r=[src.b], w=[xc.b])
                dap, dbuf = dst_fn(c, ti)
                k.op("dve", lambda: nc.vector.scalar_tensor_tensor(out=dap, in0=xc.ap, scalar=gcol.ap[:, gc0 + c:gc0 + c + 1],
                                                                  in1=rstd.ap, op0=ALU.mult, op1=ALU.mult),
                     r=[xc.b, gcol.b, rstd.b], w=[dbuf])

    def norm_tmp(pfx):
        return {"xc": [k.tile([128, TW], F32, pfx + "xc%d" % i) for i in range(3)],
                "sq": [k.tile([128, TW], F32, pfx + "sq%d" % i) for i in range(2)],
                "rstd": k.tile([128, TW], F32, pfx + "rstd")}

    def linear_fm(wc, slab_ids, rhs_fn, epilogue, banks, ntiles=NT, slab_tiles=None, preloaded=False):
        bi = 0
        kc_n = wc.kc_n
        nbuf = len(slab_tiles)
        if not preloaded:
            load_cached(slab_tiles[0], wc, slab_ids[0])
        for sp_, si in enumerate(slab_ids):
            slab = slab_tiles[sp_ % nbuf]
            if nbuf >= 2 and sp_ + 1 < len(slab_ids):
                load_cached(slab_tiles[(sp_ + 1) % nbuf], wc, slab_ids[sp_ + 1])
            for mi in range(wc.width // 128):
                for ti in range(ntiles):
                    ps = banks[bi % len(banks)]
                    bi += 1
                    for kc in range(kc_n):
                        rap, rbuf = rhs_fn(kc, ti)
                        k.op("pe", lambda: nc.tensor.matmul(ps.ap, lhsT=slab.ap[:, kc, mi * 128:(mi + 1) * 128], rhs=rap,
                                                            start=(kc == 0), stop=(kc == kc_n - 1)),
                             r=[slab.b, rbuf], w=[ps.b], inc=(kc == kc_n - 1))
                    epilogue(sp_, mi, ti, ps)

    def residual_epilogue(src, dst, tmp_x, tmp_o, chunk_fn, t0=0, order=None):
        cnt = [0]
        nx = len(tmp_x)
        depth = nx - 1 if order is not None else 0

        def issue_load(i):
            si, mi, ti = order[i]
            c = chunk_fn(si, mi)
            tok = t0 + ti * TW
            xr = tmp_x[i % nx]
            k.dma("sp", xr.ap, src.ap[c * 128:(c + 1) * 128, tok:tok + TW], r=[src.b], w=[xr.b])
        if order is not None:
            for i in range(min(depth, len(order))):
                issue_load(i)

        def ep(si, mi, ti, ps):
            c = chunk_fn(si, mi)
            tok = t0 + ti * TW
            i = cnt[0]
            xr = tmp_x[i % nx]
            ot = tmp_o[i % len(tmp_o)]
            cnt[0] += 1
            if order is None:
                k.dma("sp", xr.ap, src.ap[c * 128:(c + 1) * 128, tok:tok + TW], r=[src.b], w=[xr.b])
            else:
                assert order[i] == (si, mi, ti)
            k.op("dve", lambda: nc.vector.tensor_tensor(out=ot.ap, in0=ps.ap, in1=xr.ap, op=ALU.add),
                 r=[ps.b, xr.b], w=[ot.b])
            if order is not None and i + depth < len(order):
                issue_load(i + depth)
            k.dma("sp", dst.ap[c * 128:(c + 1) * 128, tok:tok + TW], ot.ap, r=[ot.b], w=[dst.b])
        return ep

    def fm_order(nslab, nmi, ntiles):
        return [(si, mi, ti) for si in range(nslab) for mi in range(nmi) for ti in range(ntiles)]

    def dump(tile_ap, buf, dram_tile, r0, c0, ncols):
        k.dma("sp", dram_tile.ap[r0:r0 + 128, c0:c0 + ncols], tile_ap, r=[buf], w=[dram_tile.b])

    wc_in_conv = WC("wc_in_conv", w_in[0], 16, [[(256 * i, 256), (1024 + 256 * i, 256)] for i in range(4)], 512)
    wc_in_ssm = WC("wc_in_ssm", w_in[0], 16, [[(2048 + 512 * i, 512)] for i in range(2)], 512)
    wc_glu = WC("wc_glu", w_glu[0], 8, [[(512 * i, 512)] for i in range(2)], 512)
    wc_out = WC("wc_out", w_out_even[0], 16, [[(512 * i, 512)] for i in range(4)], 512)
    wc_up = [None, None]
    wc_dn = [None, None]

    def mk_ffn_wc(l):
        wc_up[l] = WC("wc_up%d" % l, ffn_w_up[l], 16, [[(256 * i, 256), (DFF + 256 * i, 256)] for i in range(NPAIR // 2)], 512)
        wc_dn[l] = WC("wc_dn%d" % l, ffn_w_down[l], NPAIR, [[(256 * i, 256)] for i in range(8)], 256)
    mk_ffn_wc(0)
    qkv_slabs = []
    for hg in range(4):
        qkv_slabs += [[(512 * hg, 512)], [(2048 + 512 * hg, 512)], [(4096 + 512 * hg, 512)]]
    wc_qkv = WC("wc_qkv", w_qkv[0], 16, qkv_slabs, 512)
    wc_o = WC("wc_o", w_o[0], 16, [[(512 * i, 512)] for i in range(4)], 512)
    mk_ffn_wc(1)

    shared0 = {}

    def phase_transpose_in():
        KB = 1024
        gcols = k.tile([128, 64], F32, "gcols", at=14 * KB)
        hT = k.tile([128, 16, L], BF16, "hT", at=19 * KB)
        shared0["gcols"] = gcols
        shared0["hT"] = hT
        load_vec_cols(gcols, 0, ln_mix_even[0], 16)
        k.off = 83 * KB
        xin = [k.tile([128, D], F32, "xin%d" % i) for i in range(4)]
        xst = [k.tile([128, 16, TW], F32, "xst%d" % i) for i in range(2)]
        sq = [k.tile([128, TW], F32, "tsq%d" % i) for i in range(2)]
        rstd = k.tile([128, TW], F32, "trstd")
        n = 0
        k.pump_limit = wc_out.conv_end[-1]
        for tg in range(NT):
            xs = xst[tg % 2]
            for s_ in range(4):
                r0 = tg * TW + s_ * 128
                k.dma("sp", xin[s_].ap, x_in[r0:r0 + 128, :], r=[nobuf], w=[xin[s_].b])
            for c in range(16):
                ps = P[n % 4]
                n += 1
                for s_ in range(4):
                    k.op("pe", lambda: nc.tensor.transpose(out=ps.ap[:, s_ * 128:(s_ + 1) * 128], in_=xin[s_].ap[:, c * 128:(c + 1) * 128],
                                                           identity=ident),
                         r=[xin[s_].b, cf.b], w=[ps.b], inc=(s_ == 3))
                k.op("act", lambda: nc.scalar.copy(out=xs.ap[:, c, :], in_=ps.ap), r=[ps.b], w=[xs.b])
                k.dma("sp", xT[0].ap[c * 128:(c + 1) * 128, tg * TW:(tg + 1) * TW], xs.ap[:, c, :], r=[xs.b], w=[xT[0].b])
                sq_ = sq[c % 2]
                k.op("act", lambda: nc.scalar.activation(out=sq_.ap, in_=xs.ap[:, c, :], func=AF.Square), r=[xs.b], w=[sq_.b])
                k.op("pe", lambda: nc.tensor.matmul(P[7].ap, lhsT=ones_f, rhs=sq_.ap, start=(c == 0), stop=(c == 15)),
                     r=[sq_.b, cf.b], w=[P[7].b])
            k.op("act", lambda: nc.scalar.activation(out=rstd.ap, in_=P[7].ap, func=AF.Sqrt, bias=EPS, scale=1.0 / D),
                 r=[P[7].b], w=[rstd.b])
            k.op("dve", lambda: nc.vector.reciprocal(out=rstd.ap, in_=rstd.ap), r=[rstd.b], w=[rstd.b])
            for c in range(16):
                k.op("dve", lambda: nc.vector.scalar_tensor_tensor(out=hT.ap[:, c, tg * TW:(tg + 1) * TW], in0=xs.ap[:, c, :],
                                                                  scalar=gcols.ap[:, c:c + 1], in1=rstd.ap, op0=ALU.mult, op1=ALU.mult),
                     r=[xs.b, gcols.b, rstd.b], w=[hT.b])
        k.pump_limit = None
        k.off = 14 * KB

    def phase_mixer0():
        KB = 1024
        base_off = k.off
        assert base_off <= 14 * KB
        W = w_in[0]
        gcols = shared0["gcols"]
        cw = k.tile([128, 8, 31], F32, "cw", at=14 * KB + 512)
        tab = k.tile([128, 8, 32], F32, "s5tab", at=15 * KB + 512)
        stt = k.tile([128, 32, 2], F32, "s5state", at=16 * KB + 512)
        DD = k.tile([128, 8, 128], BF16, "DD", at=17 * KB)
        hT = shared0["hT"]
        slabs_t = [k.tile([128, 16, 512], BF16, "slab%d" % i, at=(83 + 16 * i) * KB) for i in range(2)]
        yaT = k.tile([128, 8, L], BF16, "yaT", at=115 * KB)
        load_vec_cols(gcols, 16, conv_b[0], 8)
        load_vec_cols(gcols, 24, conv_ln_g[0], 8)
        load_vec_cols(gcols, 32, conv_ln_b[0], 8)
        load_vec_cols(gcols, 40, ssm_d[0].rearrange("g h -> (g h)"), 8)
        load_vec_cols(gcols, 48, b_glu[0], 8)
        for c in range(8):
            k.dma("sp", cw.ap[:, c, :], conv_w[0][:, c * 128:(c + 1) * 128].rearrange("k p -> p k"), r=[nobuf], w=[cw.b])
        if "hT0" in dbg_t:
            for c in range(16):
                dump(hT.ap[:, c, :], hT.b, dbg_t["hT0"], c * 128, 0, L)
        k.off = 147 * KB
        accc = [k.tile([128, L], F32, "accc%d" % i) for i in range(2)]
        hpad = [k.tile([128, L + 32], BF16, "hpad%d" % i) for i in range(2)]
        sig = [k.tile([128, TW], F32, "sig%d" % i) for i in range(2)]
        dgs = [k.tile([128, 31, 128], BF16, "dg%d" % i) for i in range(2)]
        for hp in hpad:
            k.op("dve", lambda: nc.vector.memset(hp.ap[:, 0:32], 0.0), w=[hp.b])
        bi = 0
        cpend = []
        ncv = [0]
        load_cached(slabs_t[0], wc_in_conv, 0)
        for si in range(4):
            slab = slabs_t[si % 2]
            if si + 1 < 4:
                load_cached(slabs_t[(si + 1) % 2], wc_in_conv, si + 1)
            for lc in range(2):
                c = 2 * si + lc
                hp = hpad[c % 2]
                acc = accc[c % 2]
                dg = dgs[c % 2]
                for kk in range(31):
                    k.op("dve", lambda: nc.vector.tensor_scalar(out=dg.ap[:, kk, :], in0=ident, scalar1=cw.ap[:, c, kk:kk + 1], scalar2=None,
                                                               op0=ALU.mult), r=[cf.b, cw.b], w=[dg.b])
                for ti in range(NT):
                    psv = P[bi % 6]
                    psg = P[(bi + 1) % 6]
                    bi += 2
                    for (ps, mi) in ((psg, 2 + lc), (psv, lc)):
                        for kc in range(16):
                            k.op("pe", lambda: nc.tensor.matmul(ps.ap, lhsT=slab.ap[:, kc, mi * 128:(mi + 1) * 128],
                                                                rhs=hT.ap[:, kc, ti * TW:(ti + 1) * TW], start=(kc == 0), stop=(kc == 15)),
                                 r=[slab.b, hT.b], w=[ps.b], inc=(kc == 15))
                    if ti == 1:
                        for fn_ in cpend:
                            fn_()
                        del cpend[:]
                    sg = sig[ti % 2]
                    k.op("act", lambda: nc.scalar.activation(out=sg.ap, in_=psg.ap, func=AF.Sigmoid), r=[psg.b], w=[sg.b])
                    k.op("dve", lambda: nc.vector.tensor_tensor(out=hp.ap[:, 32 + ti * TW:32 + (ti + 1) * TW], in0=psv.ap, in1=sg.ap, op=ALU.mult),
                         r=[psv.b, sg.b], w=[hp.b])

                def conv_mm(c=c, hp=hp, acc=acc, dg=dg):
                    for ti in range(NT):
                        ps = P[6 + ncv[0] % 2]
                        ncv[0] += 1
                        for kk in range(31):
                            k.op("pe", lambda: nc.tensor.matmul(ps.ap, lhsT=dg.ap[:, kk, :], rhs=hp.ap[:, 2 + kk + ti * TW:2 + kk + (ti + 1) * TW],
                                                                start=(kk == 0), stop=(kk == 30)),
                                 r=[dg.b, hp.b], w=[ps.b], inc=(kk == 30))
                        k.op("act", lambda: nc.scalar.activation(out=acc.ap[:, ti * TW:(ti + 1) * TW], in_=ps.ap, func=AF.Identity,
                                                                 bias=gcols.ap[:, 16 + c:17 + c], scale=1.0),
                             r=[ps.b, gcols.b], w=[acc.b])
                    k.dma("sp", accD.ap[c * 128:(c + 1) * 128, :], acc.ap, r=[acc.b], w=[accD.b])
                cpend.append(conv_mm)
        for fn_ in cpend:
            fn_()
        del cpend[:]
        k.off = 147 * KB
        accr = k.tile([128, 8, TW], F32, "accr")
        sq = [k.tile([128, TW], F32, "lsq%d" % i) for i in range(2)]
        mean = k.tile([128, TW], F32, "mean")
        rstd = k.tile([128, TW], F32, "lrstd")
        tmpc = [k.tile([128, TW], F32, "ltmp%d" % i) for i in range(2)]
        for ti in range(NT):
            sl = slice(ti * TW, (ti + 1) * TW)
            k.dma("sp", accr.ap, accD.ap[:, sl].rearrange("(c p) t -> p c t", p=128), r=[accD.b], w=[accr.b])
            for c in range(8):
                k.op("pe", lambda: nc.tensor.matmul(P[0].ap, lhsT=ones_f, rhs=accr.ap[:, c, :], start=(c == 0), stop=(c == 7)),
                     r=[accr.b, cf.b], w=[P[0].b], inc=(c == 7))
            for c in range(8):
                s2 = sq[c % 2]
                k.op("act", lambda: nc.scalar.activation(out=s2.ap, in_=accr.ap[:, c, :], func=AF.Square), r=[accr.b], w=[s2.b])
                k.op("pe", lambda: nc.tensor.matmul(P[1].ap, lhsT=ones_f, rhs=s2.ap, start=(c == 0), stop=(c == 7)),
                     r=[s2.b, cf.b], w=[P[1].b])
            k.op("act", lambda: nc.scalar.mul(out=mean.ap, in_=P[0].ap, mul=1.0 / 1024), r=[P[0].b], w=[mean.b])
            t0_ = tmpc[0]
            k.op("dve", lambda: nc.vector.tensor_tensor(out=t0_.ap, in0=mean.ap, in1=mean.ap, op=ALU.mult), r=[mean.b], w=[t0_.b])
            k.op("dve", lambda: nc.vector.scalar_tensor_tensor(out=rstd.ap, in0=P[1].ap, scalar=1.0 / 1024, in1=t0_.ap,
                                                              op0=ALU.mult, op1=ALU.subtract), r=[P[1].b, t0_.b], w=[rstd.b])
            k.op("act", lambda: nc.scalar.activation(out=rstd.ap, in_=rstd.ap, func=AF.Sqrt, bias=EPS, scale=1.0), r=[rstd.b], w=[rstd.b])
            k.op("dve", lambda: nc.vector.reciprocal(out=rstd.ap, in_=rstd.ap), r=[rstd.b], w=[rstd.b])
            for c in range(8):
                tt = tmpc[c % 2]
                k.op("dve", lambda: nc.vector.tensor_tensor(out=tt.ap, in0=accr.ap[:, c, :], in1=mean.ap, op=ALU.subtract),
                     r=[accr.b, mean.b], w=[tt.b])
                k.op("dve", lambda: nc.vector.tensor_tensor(out=tt.ap, in0=tt.ap, in1=rstd.ap, op=ALU.mult), r=[tt.b, rstd.b], w=[tt.b])
                k.op("act", lambda: nc.scalar.activation(out=yaT.ap[:, c, sl], in_=tt.ap, func=AF.Silu,
                                                         bias=gcols.ap[:, 32 + c:33 + c], scale=gcols.ap[:, 24 + c:25 + c]),
                     r=[tt.b, gcols.b], w=[yaT.b])
        if "yaT" in dbg_t:
            for c in range(8):
                dump(yaT.ap[:, c, :], yaT.b, dbg_t["yaT"], c * 128, 0, L)
        if stop_after == "conv":
            return
        usT = k.tile([128, 8, L], BF16, "usT", at=147 * KB)

        def us_ep(si, mi, ti, ps):
            c = 4 * si + mi
            k.op("act", lambda: nc.scalar.copy(out=usT.ap[:, c, ti * TW:(ti + 1) * TW], in_=ps.ap), r=[ps.b], w=[usT.b])
        linear_fm(wc_in_ssm, [0, 1], lambda kc, ti: (hT.ap[:, kc, ti * TW:(ti + 1) * TW], hT.b), us_ep, P[0:4], slab_tiles=slabs_t)
        if "usT" in dbg_t:
            for c in range(8):
                dump(usT.ap[:, c, :], usT.b, dbg_t["usT"], c * 128, 0, L)
        ygT = k.tile([128, 8, L], BF16, "ygT", at=19 * KB)
        LB = k.tile([128, 32, 2, 128], BF16, "LB", at=51 * KB)
        LC = k.tile([128, 32, 2, 128], BF16, "LC", at=67 * KB)
        k.off = 179 * KB
        lr, li, stp, th, mg, fre, fim, tq = [tab.ap[:, i, :] for i in range(8)]
        tb = tab.b
        for two in range(2):
            ps_ = slice(64 * two, 64 * two + 64)
            k.dma("sp", tab.ap[ps_, 0, :], lam_re[0][two::2, :].rearrange("j p -> p j"), r=[nobuf], w=[tb])
            k.dma("sp", tab.ap[ps_, 1, :], lam_im[0][two::2, :].rearrange("j p -> p j"), r=[nobuf], w=[tb])
            k.dma("sp", tab.ap[ps_, 2, :], log_step[0][two::2].unsqueeze(0).to_broadcast([64, 32]), r=[nobuf], w=[tb])
        t2 = k.tile([128, 8, 32], F32, "s5tab2")
        cs, sn, nre, nim, den, u1, u2, u3 = [t2.ap[:, i, :] for i in range(8)]
        t2b = t2.b
        PI = math.pi

        def dv(fn, r, w):
            k.op("dve", fn, r=r, w=w)

        def ac(fn, r, w):
            k.op("act", fn, r=r, w=w)
        ac(lambda: nc.scalar.activation(out=stp, in_=stp, func=AF.Exp), [tb], [tb])
        dv(lambda: nc.vector.tensor_tensor(out=th, in0=li, in1=stp, op=ALU.mult), [tb], [tb])
        dv(lambda: nc.vector.tensor_tensor(out=tq, in0=lr, in1=stp, op=ALU.mult), [tb], [tb])
        ac(lambda: nc.scalar.activation(out=mg, in_=tq, func=AF.Exp), [tb], [tb])
        dv(lambda: nc.vector.tensor_scalar(out=th, in0=th, scalar1=1.0 / (2 * PI), scalar2=None, op0=ALU.mult), [tb], [tb])
        nis = k.tile([128, 32], I32, "s5ni")
        dv(lambda: nc.vector.tensor_copy(out=nis.ap, in_=th), [tb], [nis.b])
        dv(lambda: nc.vector.tensor_tensor(out=u1, in0=th, in1=nis.ap, op=ALU.subtract), [tb, nis.b], [t2b])
        dv(lambda: nc.vector.scalar_tensor_tensor(out=u2, in0=u1, scalar=-1.0, in1=u1, op0=ALU.mult, op1=ALU.max), [t2b], [t2b])
        ac(lambda: nc.scalar.activation(out=sn, in_=u1, func=AF.Sin, scale=TWO_PI_S), [t2b], [t2b])
        ac(lambda: nc.scalar.activation(out=cs, in_=u2, func=AF.Sin, scale=-TWO_PI_S, bias=0.5 * PI), [t2b], [t2b])
        dv(lambda: nc.vector.tensor_tensor(out=nre, in0=mg, in1=cs, op=ALU.mult), [tb, t2b], [t2b])
        dv(lambda: nc.vector.tensor_scalar(out=nre, in0=nre, scalar1=-1.0, scalar2=None, op0=ALU.add), [t2b], [t2b])
        dv(lambda: nc.vector.tensor_tensor(out=nim, in0=mg, in1=sn, op=ALU.mult), [tb, t2b], [t2b])
        dv(lambda: nc.vector.tensor_tensor(out=den, in0=lr, in1=lr, op=ALU.mult), [tb], [t2b])
        dv(lambda: nc.vector.tensor_tensor(out=u3, in0=li, in1=li, op=ALU.mult), [tb], [t2b])
        dv(lambda: nc.vector.tensor_tensor(out=den, in0=den, in1=u3, op=ALU.add), [t2b], [t2b])
        dv(lambda: nc.vector.reciprocal(out=den, in_=den), [t2b], [t2b])
        dv(lambda: nc.vector.tensor_tensor(out=u1, in0=nre, in1=lr, op=ALU.mult), [tb, t2b], [t2b])
        dv(lambda: nc.vector.tensor_tensor(out=u2, in0=nim, in1=li, op=ALU.mult), [tb, t2b], [t2b])
        dv(lambda: nc.vector.tensor_tensor(out=u1, in0=u1, in1=u2, op=ALU.add), [t2b], [t2b])
        dv(lambda: nc.vector.tensor_tensor(out=fre, in0=u1, in1=den, op=ALU.mult), [t2b], [tb])
        dv(lambda: nc.vector.tensor_tensor(out=u1, in0=nim, in1=lr, op=ALU.mult), [tb, t2b], [t2b])
        dv(lambda: nc.vector.tensor_tensor(out=u2, in0=nre, in1=li, op=ALU.mult), [tb, t2b], [t2b])
        dv(lambda: nc.vector.tensor_tensor(out=u1, in0=u1, in1=u2, op=ALU.subtract), [t2b], [t2b])
        dv(lambda: nc.vector.tensor_tensor(out=fim, in0=u1, in1=den, op=ALU.mult), [t2b], [tb])
        braw = [k.tile([128, 32, 16], F32, "braw%d" % i) for i in range(2)]
        for ri, bsrc in enumerate((b_re, b_im)):
            for two in range(2):
                k.dma("sp", braw[ri].ap[64 * two:64 * two + 64], bsrc[0][two::2].rearrange("j p h -> p j h"),
                      r=[nobuf], w=[braw[ri].b])
        bb = [k.tile([128, 32, 16], F32, "bb%d" % i) for i in range(2)]
        btmp = k.tile([128, 32, 16], F32, "btmp")
        fre_b = tab.ap[:, 5, :].unsqueeze(2).to_broadcast([128, 32, 16])
        fim_b = tab.ap[:, 6, :].unsqueeze(2).to_broadcast([128, 32, 16])
        dv(lambda: nc.vector.tensor_tensor(out=bb[0].ap, in0=braw[0].ap, in1=fre_b, op=ALU.mult), [braw[0].b, tb], [bb[0].b])
        dv(lambda: nc.vector.tensor_tensor(out=btmp.ap, in0=braw[1].ap, in1=fim_b, op=ALU.mult), [braw[1].b, tb], [btmp.b])
        dv(lambda: nc.vector.tensor_tensor(out=bb[0].ap, in0=bb[0].ap, in1=btmp.ap, op=ALU.subtract), [bb[0].b, btmp.b], [bb[0].b])
        dv(lambda: nc.vector.tensor_tensor(out=bb[1].ap, in0=braw[1].ap, in1=fre_b, op=ALU.mult), [braw[1].b, tb], [bb[1].b])
        dv(lambda: nc.vector.tensor_tensor(out=btmp.ap, in0=braw[0].ap, in1=fim_b, op=ALU.mult), [braw[0].b, tb], [btmp.b])
        dv(lambda: nc.vector.tensor_tensor(out=bb[1].ap, in0=bb[1].ap, in1=btmp.ap, op=ALU.add), [bb[1].b, btmp.b], [bb[1].b])
        pad = k.tile([128, 32, 2, 128], F32, "s5pad", at=83 * KB)
        dv(lambda: nc.vector.memset(pad.ap, 0.0), [], [pad.b])
        for ri in range(2):
            for two in range(2):
                for jm in range(4):
                    col = 32 * jm + 16 * two
                    o_ap = pad.ap[64 * two:64 * two + 64, :, ri, col:col + 16].rearrange("p (c m) h -> p c m h", m=4)[:, :, jm, :]
                    i_ap = bb[ri].ap[64 * two:64 * two + 64].rearrange("p (c m) h -> p c m h", m=4)[:, :, jm, :]
                    dv(lambda: nc.vector.tensor_copy(out=o_ap, in_=i_ap), [bb[ri].b], [pad.b])

        def transpose_pad(dst):
            n = 0
            for j in range(0, 32, 2):
                ps = P[n % 4]
                n += 1
                for jj in range(2):
                    for ri in range(2):
                        q = jj * 2 + ri
                        k.op("pe", lambda: nc.tensor.transpose(out=ps.ap[:, q * 128:(q + 1) * 128], in_=pad.ap[:, j + jj, ri, :], identity=ident),
                             r=[pad.b, cf.b], w=[ps.b], inc=(q == 3))
                k.op("act", lambda: nc.scalar.copy(out=dst.ap[:, j:j + 2].rearrange("p a b c -> p (a b c)"), in_=ps.ap), r=[ps.b], w=[dst.b])
        transpose_pad(LB)
        craw = [k.tile([128, 8, 64], F32, "craw%d" % i) for i in range(2)]
        for ri, csrc in enumerate((c_re, c_im)):
            k.dma("sp", craw[ri].ap, csrc[0].rearrange("(c gm) ho p -> (gm ho) c p", gm=8), r=[nobuf], w=[craw[ri].b])
        for ri in range(2):
            for two in range(2):
                for jm in range(4):
                    o_ap = pad.ap[:, :, ri, 64 * two:64 * two + 64].rearrange("p (c m) s -> p c m s", m=4)[:, :, jm, :]
                    g8 = 2 * jm + two
                    dv(lambda: nc.vector.tensor_scalar(out=o_ap, in0=craw[ri].ap, scalar1=RM[:, g8:g8 + 1], scalar2=(-1.0 if ri else 1.0),
                                                       op0=ALU.mult, op1=ALU.mult), [craw[ri].b, cf.b], [pad.b])
        transpose_pad(LC)
        for c in range(8):
            dv(lambda: nc.vector.tensor_scalar(out=DD.ap[:, c, :], in0=ident, scalar1=gcols.ap[:, 40 + c:41 + c], scalar2=None, op0=ALU.mult),
               [cf.b, gcols.b], [DD.b])
        dv(lambda: nc.vector.memset(stt.ap, 0.0), [], [stt.b])
        om = k.tile([128, 4, 32], F32, "s5om", at=179 * KB)
        nio = k.tile([128, 32], I32, "s5nio", at=179 * KB + 2048)
        dv(lambda: nc.vector.tensor_scalar(out=om.ap[:, 2, :], in0=tab.ap[:, 3, :], scalar1=512.0, scalar2=None, op0=ALU.mult), [tb], [om.b])
        dv(lambda: nc.vector.tensor_copy(out=nio.ap, in_=om.ap[:, 2, :]), [om.b], [nio.b])
        dv(lambda: nc.vector.tensor_tensor(out=om.ap[:, 2, :], in0=om.ap[:, 2, :], in1=nio.ap, op=ALU.subtract), [om.b, nio.b], [om.b])
        dv(lambda: nc.vector.scalar_tensor_tensor(out=om.ap[:, 3, :], in0=om.ap[:, 2, :], scalar=-1.0, in1=om.ap[:, 2, :],
                                                  op0=ALU.mult, op1=ALU.max), [om.b], [om.b])
        ac(lambda: nc.scalar.activation(out=om.ap[:, 1, :], in_=om.ap[:, 2, :], func=AF.Sin, scale=TWO_PI_S), [om.b], [om.b])
        ac(lambda: nc.scalar.activation(out=om.ap[:, 0, :], in_=om.ap[:, 3, :], func=AF.Sin, scale=-TWO_PI_S, bias=0.5 * PI), [om.b], [om.b])
        sti = k.tile([128, 4, 2], F32, "s5sti", at=179 * KB + 2304)
        rtm = k.tile([128, 4, 4], F32, "s5rtm", at=179 * KB + 2368)
        k.off = 83 * KB
        ncs4 = [k.tile([128, 2, TW], BF16, "ncs%d" % i) for i in range(4)]
        bub = [k.tile([128, 2, TW], BF16, "bub%d" % i) for i in range(2)]
        xt_ = [k.tile([128, 2, TW], BF16, "xt%d" % i) for i in range(2)]
        sc_ = [k.tile([128, 2, TW], BF16, "sc%d" % i) for i in range(2)]
        tmD = k.tile([128, 2, TW], BF16, "tmD")
        tmP = k.tile([128, 2, TW], BF16, "tmP")
        sb_ = [k.tile([128, 2, TW], BF16, "sb%d" % i) for i in range(2)]
        k.off = 182 * KB
        ph = k.tile([128, 2, TW], F32, "ph")
        nit = k.tile([128, TW], I32, "nit")
        ysb = k.tile([128, TW], F32, "ysb")
        yt = [k.tile([128, TW], F32, "yt%d" % i) for i in range(2)]
        it = 0
        GC = 2.0 * math.sqrt(2.0 / math.pi)

        def pl(fn, r, w):
            k.op("pool", fn, r=r, w=w)
        pend = []
        for c in range(8):
            for jm in range(4):
                j = 4 * c + jm
                t_ = ncs4[jm]
                dv(lambda: nc.vector.tensor_scalar(out=ph.ap[:, 1, :], in0=tvec[:, 0:TW], scalar1=tab.ap[:, 3, j:j + 1], scalar2=None,
                                                   op0=ALU.mult), [cf.b, tb], [ph.b])
                dv(lambda: nc.vector.tensor_copy(out=nit.ap, in_=ph.ap[:, 1, :]), [ph.b], [nit.b])
                dv(lambda: nc.vector.tensor_tensor(out=ph.ap[:, 1, :], in0=ph.ap[:, 1, :], in1=nit.ap, op=ALU.subtract),
                   [ph.b, nit.b], [ph.b])
                dv(lambda: nc.vector.scalar_tensor_tensor(out=ph.ap[:, 0, :], in0=ph.ap[:, 1, :], scalar=-1.0, in1=ph.ap[:, 1, :],
                                                          op0=ALU.mult, op1=ALU.max), [ph.b], [ph.b])
                ac(lambda: nc.scalar.activation(out=t_.ap[:, 1, :], in_=ph.ap[:, 1, :], func=AF.Sin, scale=TWO_PI_S), [ph.b], [t_.b])
                ac(lambda: nc.scalar.activation(out=t_.ap[:, 0, :], in_=ph.ap[:, 0, :], func=AF.Sin, scale=-TWO_PI_S, bias=0.5 * PI),
                   [ph.b], [t_.b])
            dv(lambda: nc.vector.memset(sti.ap, 0.0), [], [sti.b])
            for qd in range(NT):
                sl = slice(qd * TW, (qd + 1) * TW)
                if qd > 0:
                    sr_, si2 = stt.ap[:, 4 * c:4 * c + 4, 0], stt.ap[:, 4 * c:4 * c + 4, 1]
                    oc, os_ = om.ap[:, 0, 4 * c:4 * c + 4], om.ap[:, 1, 4 * c:4 * c + 4]
                    dv(lambda: nc.vector.tensor_tensor(out=rtm.ap[:, 0, :], in0=sr_, in1=oc, op=ALU.mult), [stt.b, om.b], [rtm.b])
                    dv(lambda: nc.vector.tensor_tensor(out=rtm.ap[:, 1, :], in0=si2, in1=os_, op=ALU.mult), [stt.b, om.b], [rtm.b])
                    dv(lambda: nc.vector.tensor_tensor(out=rtm.ap[:, 2, :], in0=sr_, in1=os_, op=ALU.mult), [stt.b, om.b], [rtm.b])
                    dv(lambda: nc.vector.tensor_tensor(out=rtm.ap[:, 3, :], in0=si2, in1=oc, op=ALU.mult), [stt.b, om.b], [rtm.b])
                    dv(lambda: nc.vector.tensor_tensor(out=sti.ap[:, :, 0], in0=rtm.ap[:, 0, :], in1=rtm.ap[:, 1, :], op=ALU.subtract),
                       [rtm.b], [sti.b])
                    dv(lambda: nc.vector.tensor_tensor(out=sti.ap[:, :, 1], in0=rtm.ap[:, 2, :], in1=rtm.ap[:, 3, :], op=ALU.add),
                       [rtm.b], [sti.b])
                yps = P[6 + (c * NT + qd) % 2]
                k.op("pe", lambda: nc.tensor.matmul(yps.ap, lhsT=DD.ap[:, c, :], rhs=usT.ap[:, c, sl], start=True, stop=False),
                     r=[DD.b, usT.b], w=[yps.b], inc=False)
                for jm in range(4):
                    j = 4 * c + jm
                    u = it % 2
                    it += 1
                    pr, pi_ = P[(it % 3) * 2], P[(it % 3) * 2 + 1]
                    for ri, pp in ((0, pr), (1, pi_)):
                        k.op("pe", lambda: nc.tensor.matmul(pp.ap, lhsT=LB.ap[:, j, ri, :], rhs=usT.ap[:, c, sl], start=True, stop=True),
                             r=[LB.b, usT.b], w=[pp.b])
                    for fn_ in pend:
                        fn_()
                    del pend[:]
                    cs_t = ncs4[jm]
                    ncos, nsin = cs_t.ap[:, 0, :], cs_t.ap[:, 1, :]
                    ac(lambda: nc.scalar.copy(out=bub[u].ap[:, 0, :], in_=pr.ap), [pr.b], [bub[u].b])
                    ac(lambda: nc.scalar.copy(out=bub[u].ap[:, 1, :], in_=pi_.ap), [pi_.b], [bub[u].b])
                    br_, bi_ = bub[u].ap[:, 0, :], bub[u].ap[:, 1, :]
                    dv(lambda: nc.vector.tensor_tensor(out=xt_[u].ap[:, 0, :], in0=br_, in1=ncos, op=ALU.mult), [bub[u].b, cs_t.b], [xt_[u].b])
                    dv(lambda: nc.vector.tensor_tensor(out=tmD.ap[:, 0, :], in0=bi_, in1=nsin, op=ALU.mult), [bub[u].b, cs_t.b], [tmD.b])
                    dv(lambda: nc.vector.tensor_tensor(out=xt_[u].ap[:, 1, :], in0=bi_, in1=ncos, op=ALU.mult), [bub[u].b, cs_t.b], [xt_[u].b])
                    dv(lambda: nc.vector.tensor_tensor(out=tmD.ap[:, 1, :], in0=br_, in1=nsin, op=ALU.mult), [bub[u].b, cs_t.b], [tmD.b])
                    dv(lambda: nc.vector.tensor_tensor(out=xt_[u].ap[:, 0, :], in0=xt_[u].ap[:, 0, :], in1=tmD.ap[:, 0, :], op=ALU.add),
                       [xt_[u].b, tmD.b], [xt_[u].b])
                    dv(lambda: nc.vector.tensor_tensor(out=xt_[u].ap[:, 1, :], in0=xt_[u].ap[:, 1, :], in1=tmD.ap[:, 1, :], op=ALU.subtract),
                       [xt_[u].b, tmD.b], [xt_[u].b])
                    a_b = tab.ap[:, 4, j:j + 1].to_broadcast([128, TW])
                    for ri in range(2):
                        dv(lambda: nc.vector.tensor_tensor_scan(out=sc_[u].ap[:, ri, :], data0=a_b, data1=xt_[u].ap[:, ri, :],
                                                                initial=sti.ap[:, jm, ri:ri + 1], op0=ALU.mult, op1=ALU.add),
                           [tb, xt_[u].b, sti.b], [sc_[u].b])
                    dv(lambda: nc.vector.tensor_copy(out=stt.ap[:, j, :], in_=sc_[u].ap[:, :, TW - 1]), [sc_[u].b], [stt.b])
                    sr, si_ = sc_[u].ap[:, 0, :], sc_[u].ap[:, 1, :]
                    dv(lambda: nc.vector.tensor_tensor(out=tmP.ap[:, 0, :], in0=sr, in1=ncos, op=ALU.mult), [sc_[u].b, cs_t.b], [tmP.b])
                    dv(lambda: nc.vector.tensor_tensor(out=tmP.ap[:, 1, :], in0=si_, in1=nsin, op=ALU.mult), [sc_[u].b, cs_t.b], [tmP.b])
                    dv(lambda: nc.vector.tensor_tensor(out=sb_[u].ap[:, 0, :], in0=tmP.ap[:, 0, :], in1=tmP.ap[:, 1, :], op=ALU.subtract),
                       [tmP.b], [sb_[u].b])
                    dv(lambda: nc.vector.tensor_tensor(out=tmP.ap[:, 0, :], in0=sr, in1=nsin, op=ALU.mult), [sc_[u].b, cs_t.b], [tmP.b])
                    dv(lambda: nc.vector.tensor_tensor(out=tmP.ap[:, 1, :], in0=si_, in1=ncos, op=ALU.mult), [sc_[u].b, cs_t.b], [tmP.b])
                    dv(lambda: nc.vector.tensor_tensor(out=sb_[u].ap[:, 1, :], in0=tmP.ap[:, 0, :], in1=tmP.ap[:, 1, :], op=ALU.add),
                       [tmP.b], [sb_[u].b])

                    def ymm(j=j, u=u, jm=jm, yps=yps):
                        for ri in range(2):
                            last = (jm == 3 and ri == 1)
                            k.op("pe", lambda: nc.tensor.matmul(yps.ap, lhsT=LC.ap[:, j, ri, :], rhs=sb_[u].ap[:, ri, :], start=False, stop=last),
                                 r=[LC.b, sb_[u].b], w=[yps.b], inc=True)
                    pend.append(ymm)

                def gelu_ep(c=c, sl=sl, yps=yps):
                    ac(lambda: nc.scalar.copy(out=ysb.ap, in_=yps.ap), [yps.b], [ysb.b])
                    dv(lambda: nc.vector.tensor_tensor(out=yt[0].ap, in0=ysb.ap, in1=ysb.ap, op=ALU.mult), [ysb.b], [yt[0].b])
                    dv(lambda: nc.vector.tensor_scalar(out=yt[0].ap, in0=yt[0].ap, scalar1=0.044715, scalar2=1.0, op0=ALU.mult, op1=ALU.add),
                       [yt[0].b], [yt[0].b])
                    dv(lambda: nc.vector.tensor_tensor(out=yt[0].ap, in0=yt[0].ap, in1=ysb.ap, op=ALU.mult), [yt[0].b, ysb.b], [yt[0].b])
                    ac(lambda: nc.scalar.activation(out=yt[1].ap, in_=yt[0].ap, func=AF.Sigmoid, scale=GC), [yt[0].b], [yt[1].b])
                    dv(lambda: nc.vector.tensor_tensor(out=ygT.ap[:, c, sl], in0=ysb.ap, in1=yt[1].ap, op=ALU.mult), [ysb.b, yt[1].b], [ygT.b])
                pend.append(gelu_ep)
        for fn_ in pend:
            fn_()
        del pend[:]
        k.off = 83 * KB
        ybT = usT
        gsl = [k.tile([128, 8, 512], BF16, "gslab%d" % i) for i in range(2)]
        sg2 = [k.tile([128, TW], F32, "sg2_%d" % i) for i in range(2)]
        cnt = [0]

        def glu_ep(si, mi, ti, ps):
            c = 4 * si + mi
            s = sg2[cnt[0] % 2]
            cnt[0] += 1
            sl = slice(ti * TW, (ti + 1) * TW)
            ac(lambda: nc.scalar.activation(out=s.ap, in_=ps.ap, func=AF.Sigmoid, bias=gcols.ap[:, 48 + c:49 + c], scale=1.0),
               [ps.b, gcols.b], [s.b])
            dv(lambda: nc.vector.tensor_tensor(out=ybT.ap[:, c, sl], in0=ygT.ap[:, c, sl], in1=s.ap, op=ALU.mult), [ygT.b, s.b], [ybT.b])
        linear_fm(wc_glu, [0, 1], lambda kc, ti: (ygT.ap[:, kc, ti * TW:(ti + 1) * TW], ygT.b), glu_ep, P[0:4], slab_tiles=gsl)
        if "ybT" in dbg_t:
            for c in range(8):
                dump(ybT.ap[:, c, :], ybT.b, dbg_t["ybT"], c * 128, 0, L)
        slabs_t = [k.tile([128, 16, 512], BF16, "oslab%d" % i, at=(83 + 16 * i) * KB) for i in range(2)]
        k.off = 179 * KB
        tx = [k.tile([128, TW], F32, "tx%d" % i) for i in range(3)]
        to = [k.tile([128, TW], F32, "to%d" % i) for i in range(3)]

        def cat_rhs(kc, ti):
            sl = slice(ti * TW, (ti + 1) * TW)
            if kc < 8:
                return yaT.ap[:, kc, sl], yaT.b
            return ybT.ap[:, kc - 8, sl], ybT.b
        linear_fm(wc_out, [0, 1, 2, 3], cat_rhs,
                  residual_epilogue(xT[0], xT[1], tx, to, lambda si, mi: 4 * si + mi, order=fm_order(4, 4, NT)), P[4:8], slab_tiles=slabs_t)
        k.off = base_off

    def phase_ffn(layer, src, dst):
        m0 = k.mark()
        wu = wc_up[layer]
        wdn = wc_dn[layer]
        gcol = k.tile([128, 16], F32, "fgcol")
        load_vec_cols(gcol, 0, ln_ffn[layer], 16)
        fcw = k.tile([128, 88, 3], F32, "fcw")
        for kk in range(3):
            for c0 in range(0, 88, 22):
                k.dma("sp", fcw.ap[:, c0:c0 + 22, kk], ffn_conv_w[layer][kk, c0 * 128:(c0 + 22) * 128].rearrange("(c p) -> p c", p=128),
                      r=[nobuf], w=[fcw.b])
        carry = k.tile([128, 88, 2], F32, "carry")
        k.op("dve", lambda: nc.vector.memset(carry.ap, 0.0), w=[carry.b])
        hbs = [k.tile([128, 16, TW], BF16, "hb%d" % i) for i in range(2)]
        gT = k.tile([128, NPAIR, TW], BF16, "gT")
        NUB = 2
        uslab = [k.tile([128, 16, 512], BF16, "uslab%d" % i) for i in range(NUB)]
        dslab = [k.tile([128, NPAIR, 256], BF16, "dslab%d" % i) for i in range(2)]
        tmp = norm_tmp("f")
        upad = [k.tile([128, TW + 2], F32, "upad%d" % i) for i in range(4)]
        accg = [k.tile([128, TW], F32, "accg%d" % i) for i in range(2)]
        accv = [k.tile([128, TW], F32, "accv%d" % i) for i in range(2)]
        tx = [k.tile([128, TW], F32, "ftx%d" % i) for i in range(2)]
        to = [k.tile([128, TW], F32, "fto%d" % i) for i in range(2)]
        nup = 0
        nb = 0
        nsl = 0
        load_cached(uslab[0], wu, 0)
        rms_norm(src, gcol, 0, lambda c, ti: (hbs[0].ap[:, c, :], hbs[0].b), 0, 1, P[5], tmp)
        for blk in range(NT):
            t0 = blk * TW
            hb = hbs[blk % 2]
            NS2 = NPAIR // 2
            for s2 in range(NS2):
                slab = uslab[nsl % NUB]
                nxt = s2 + NUB - 1
                if nxt < NS2:
                    load_cached(uslab[(nsl + NUB - 1) % NUB], wu, nxt)
                elif blk + 1 < NT:
                    load_cached(uslab[(nsl + NUB - 1) % NUB], wu, nxt - NS2)
                nsl += 1
                if s2 == 8 and blk + 1 < NT:
                    hn = hbs[(blk + 1) % 2]
                    rms_norm(src, gcol, 0, lambda c, ti: (hn.ap[:, c, :], hn.b), t0 + TW, 1, P[5], tmp)
                if s2 == 17:
                    load_cached(dslab[0], wdn, 0)
                for lp in range(2):
                    pair = 2 * s2 + lp
                    fg, fv = pair, NPAIR + pair
                    psg, psv = P[nb % 5], P[(nb + 1) % 5]
                    nb += 2
                    upg, upv = upad[nup % 4], upad[(nup + 1) % 4]
                    nup += 2
                    ag, av = accg[pair % 2], accv[pair % 2]
                    for (ps, mi) in ((psg, lp), (psv, 2 + lp)):
                        for kc in range(16):
                            k.op("pe", lambda: nc.tensor.matmul(ps.ap, lhsT=slab.ap[:, kc, mi * 128:(mi + 1) * 128], rhs=hb.ap[:, kc, :],
                                                                start=(kc == 0), stop=(kc == 15)),
                                 r=[slab.b, hb.b], w=[ps.b], inc=(kc == 15))
                    for (ps, up, f) in ((psg, upg, fg), (psv, upv, fv)):
                        k.op("act", lambda: nc.scalar.copy(out=up.ap[:, 0:2], in_=carry.ap[:, f, :]), r=[carry.b], w=[up.b])
                        k.op("act", lambda: nc.scalar.copy(out=up.ap[:, 2:TW + 2], in_=ps.ap), r=[ps.b], w=[up.b])
                    for (up, f) in ((upg, fg), (upv, fv)):
                        k.op("act", lambda: nc.scalar.copy(out=carry.ap[:, f, :], in_=up.ap[:, TW:TW + 2]), r=[up.b], w=[carry.b])
                    for tap in (2, 1, 0):
                        for (up, f, a) in ((upg, fg, ag), (upv, fv, av)):
                            if tap == 2:
                                k.op("dve", lambda: nc.vector.tensor_scalar(out=a.ap, in0=up.ap[:, 2:TW + 2], scalar1=fcw.ap[:, f, 2:3],
                                                                           scalar2=None, op0=ALU.mult), r=[up.b, fcw.b], w=[a.b])
                            else:
                                k.op("dve", lambda: nc.vector.scalar_tensor_tensor(out=a.ap, in0=up.ap[:, tap:TW + tap],
                                                                                  scalar=fcw.ap[:, f, tap:tap + 1], in1=a.ap,
                                                                                  op0=ALU.mult, op1=ALU.add), r=[up.b, fcw.b, a.b], w=[a.b])
                    k.op("act", lambda: nc.scalar.activation(out=ag.ap, in_=ag.ap, func=AF.Silu), r=[ag.b], w=[ag.b])
                    k.op("dve", lambda: nc.vector.tensor_tensor(out=gT.ap[:, pair, :], in0=ag.ap, in1=av.ap, op=ALU.mult),
                         r=[ag.b, av.b], w=[gT.b])
            linear_fm(wdn, list(range(8)), lambda kc, ti: (gT.ap[:, kc, :], gT.b),
                      residual_epilogue(src, dst, tx, to, lambda si, mi: 2 * si + mi, t0=t0, order=fm_order(8, 2, 1)), [P[6], P[7]], ntiles=1, slab_tiles=dslab,
                      preloaded=True)
        k.release(m0)

    def phase_attn(src, dst):
        m0 = k.mark()
        Wq = w_qkv[0]
        gcol = k.tile([128, 16], F32, "agcol")
        load_vec_cols(gcol, 0, ln_mix_odd[0], 16)
        hT = k.tile([128, 16, L], BF16, "ahT")
        m1 = k.mark()
        tmp = norm_tmp("a")
        rms_norm(src, gcol, 0, lambda c, ti: (hT.ap[:, c, ti * TW:(ti + 1) * TW], hT.b), 0, NT, P[7], tmp)
        k.release(m1)
        slabs_t = [k.tile([128, 16, 512], BF16, "aslab%d" % i) for i in range(2)]
        qT = k.tile([128, 4, L], BF16, "qT")
        kT = k.tile([128, 4, L], BF16, "kT")
        vS = k.tile([128, 16, 512], BF16, "vS")
        oh = [k.tile([128, L], BF16, "oh%d" % i) for i in range(2)]
        NB3 = 3
        e_t = [k.tile([128, TW], F32, "e_t%d" % i) for i in range(NB3)]
        sp_t = [k.tile([128, TW], BF16, "sp_t%d" % i) for i in range(NB3)]
        arg_t = [k.tile([128, TW], F32, "arg_t%d" % i) for i in range(NB3)]
        w_t = [k.tile([128, TW], BF16, "w_t%d" % i) for i in range(NB3)]
        Rts = [k.tile([128, TW], F32, "Rt%d" % i) for i in range(2)]
        scale = 128.0 ** -0.5
        hrhs = lambda kc, ti: (hT.ap[:, kc, ti * TW:(ti + 1) * TW], hT.b)
        nsl = 0
        for hg in range(4):
            def q_ep(si, mi, ti, ps):
                k.op("act", lambda: nc.scalar.mul(out=qT.ap[:, mi, ti * TW:(ti + 1) * TW], in_=ps.ap, mul=scale), r=[ps.b], w=[qT.b])

            def k_ep(si, mi, ti, ps):
                k.op("dve", lambda: nc.vector.tensor_copy(out=kT.ap[:, mi, ti * TW:(ti + 1) * TW], in_=ps.ap), r=[ps.b], w=[kT.b])
            if hg == 0:
                load_cached(slabs_t[0], wc_qkv, 0)
            load_cached(slabs_t[(nsl + 1) % 2], wc_qkv, 3 * hg + 1)
            linear_fm(wc_qkv, [3 * hg], hrhs, q_ep, P[0:4], slab_tiles=[slabs_t[nsl % 2]], preloaded=True)
            nsl += 1
            load_cached(slabs_t[(nsl + 1) % 2], wc_qkv, 3 * hg + 2)
            linear_fm(wc_qkv, [3 * hg + 1], hrhs, k_ep, P[0:4], slab_tiles=[slabs_t[nsl % 2]], preloaded=True)
            nsl += 1
            vsl = slabs_t[nsl % 2]
            nsl += 1
            for tt in range(16):
                ps = P[tt % 4]
                for kc in range(16):
                    k.op("pe", lambda: nc.tensor.matmul(ps.ap, lhsT=hT.ap[:, kc, tt * 128:(tt + 1) * 128], rhs=vsl.ap[:, kc, :],
                                                        start=(kc == 0), stop=(kc == 15)), r=[hT.b, vsl.b], w=[ps.b], inc=(kc == 15))
                k.op("act", lambda: nc.scalar.copy(out=vS.ap[:, tt, :], in_=ps.ap), r=[ps.b], w=[vS.b])
            if hg + 1 < 4:
                load_cached(slabs_t[nsl % 2], wc_qkv, 3 * (hg + 1))
            tiles = []
            seqi = 0
            for hl in range(4):
                for J in range(NT):
                    nblk = 4 * J + 4
                    for b in range(nblk - 1, -1, -1):
                        tiles.append((hl, J, b, nblk, seqi))
                    seqi += 1
            NTL = len(tiles)

            def stA(n):
                hl, J, b, nblk, sq_ = tiles[n]
                zps = P[n % 2]
                dg = b >= 4 * J
                k.op("pe", lambda: nc.tensor.matmul(zps.ap, lhsT=kT.ap[:, hl, b * 128:(b + 1) * 128], rhs=qT.ap[:, hl, J * TW:(J + 1) * TW],
                                                    start=True, stop=not dg), r=[kT.b, qT.b], w=[zps.b], inc=not dg)
                if dg:
                    k.op("pe", lambda: nc.tensor.matmul(zps.ap, lhsT=ident_bf, rhs=masks[b - 4 * J], start=False, stop=True),
                         r=[cb.b], w=[zps.b])

            def stB(n):
                hl, J, b, nblk, sq_ = tiles[n]
                zps = P[n % 2]
                u = n % NB3
                k.op("act", lambda: nc.scalar.activation(out=e_t[u].ap, in_=zps.ap, func=AF.Exp), r=[zps.b], w=[e_t[u].b])
                k.op("act", lambda: nc.scalar.activation(out=sp_t[u].ap, in_=e_t[u].ap, func=AF.Ln, bias=1.0, scale=1.0),
                     r=[e_t[u].b], w=[sp_t[u].b])

            def stC(n):
                hl, J, b, nblk, sq_ = tiles[n]
                u = n % NB3
                aps, cps = P[2 + n % 2], P[4 + n % 2]
                k.op("pe", lambda: nc.tensor.matmul(aps.ap, lhsT=tri_neg, rhs=sp_t[u].ap, start=True, stop=False),
                     r=[cb.b, sp_t[u].b], w=[aps.b], inc=False)
                dg = b >= 4 * J
                k.op("pe", lambda: nc.tensor.matmul(aps.ap, lhsT=kT.ap[:, hl, b * 128:(b + 1) * 128], rhs=qT.ap[:, hl, J * TW:(J + 1) * TW],
                                                    start=False, stop=not dg), r=[kT.b, qT.b], w=[aps.b], inc=not dg)
                if dg:
                    k.op("pe", lambda: nc.tensor.matmul(aps.ap, lhsT=ident_bf, rhs=masks[b - 4 * J], start=False, stop=True),
                         r=[cb.b], w=[aps.b])
                k.op("pe", lambda: nc.tensor.matmul(cps.ap, lhsT=ones_neg, rhs=sp_t[u].ap, start=True, stop=True),
                     r=[cb.b, sp_t[u].b], w=[cps.b])

            def stD(n):
                hl, J, b, nblk, sq_ = tiles[n]
                u = n % NB3
                aps, cps = P[2 + n % 2], P[4 + n % 2]
                Rt = Rts[sq_ % 2]
                if b == nblk - 1:
                    k.op("act", lambda: nc.scalar.activation(out=w_t[u].ap, in_=aps.ap, func=AF.Exp), r=[aps.b], w=[w_t[u].b])
                    k.op("dve", lambda: nc.vector.tensor_copy(out=Rt.ap, in_=cps.ap), r=[cps.b], w=[Rt.b])
                else:
                    k.op("dve", lambda: nc.vector.tensor_tensor(out=arg_t[u].ap, in0=aps.ap, in1=Rt.ap, op=ALU.add),
                         r=[aps.b, Rt.b], w=[arg_t[u].b])
                    if b > 0:
                        k.op("dve", lambda: nc.vector.tensor_tensor(out=Rt.ap, in0=cps.ap, in1=Rt.ap, op=ALU.add),
                             r=[cps.b, Rt.b], w=[Rt.b])
                    k.op("act", lambda: nc.scalar.activation(out=w_t[u].ap, in_=arg_t[u].ap, func=AF.Exp), r=[arg_t[u].b], w=[w_t[u].b])

            def stE(n):
                hl, J, b, nblk, sq_ = tiles[n]
                u = n % NB3
                ops_ = P[6 + sq_ % 2]
                k.op("pe", lambda: nc.tensor.matmul(ops_.ap, lhsT=vS.ap[:, b, hl * 128:(hl + 1) * 128], rhs=w_t[u].ap,
                                                    start=(b == nblk - 1), stop=(b == 0)), r=[vS.b, w_t[u].b], w=[ops_.b], inc=True)
                if b == 0:
                    h = 4 * hg + hl
                    ohh = oh[h % 2]
                    k.op("act", lambda: nc.scalar.copy(out=ohh.ap[:, J * TW:(J + 1) * TW], in_=ops_.ap), r=[ops_.b], w=[ohh.b])
                    if J == NT - 1:
                        k.dma("sp", oT_d.ap[h * 128:(h + 1) * 128, :], ohh.ap, r=[ohh.b], w=[oT_d.b])

            stA(0)
            for n in range(NTL + 2):
                if n + 1 < NTL:
                    stA(n + 1)
                if n < NTL:
                    stB(n)
                if 0 <= n - 1 < NTL:
                    stC(n - 1)
                    stD(n - 1)
                if 0 <= n - 2 < NTL:
                    stE(n - 2)
        k.release(m0)
        m2 = k.mark()
        oT = k.tile([128, 16, L], BF16, "oT")
        for c in range(16):
            k.dma("sp", oT.ap[:, c, :], oT_d.ap[c * 128:(c + 1) * 128, :], r=[oT_d.b], w=[oT.b])
        slabs_o = [k.tile([128, 16, 512], BF16, "oslab%d" % i) for i in range(2)]
        tx = [k.tile([128, TW], F32, "otx%d" % i) for i in range(3)]
        to = [k.tile([128, TW], F32, "oto%d" % i) for i in range(3)]
        linear_fm(wc_o, [0, 1, 2, 3], lambda kc, ti: (oT.ap[:, kc, ti * TW:(ti + 1) * TW], oT.b),
                  residual_epilogue(src, dst, tx, to, lambda si, mi: 4 * si + mi, order=fm_order(4, 4, NT)), P[0:4], slab_tiles=slabs_o)
        k.release(m2)

    def phase_final(src):
        m0 = k.mark()
        gcol = k.tile([128, 16], F32, "zgcol")
        load_vec_cols(gcol, 0, ln_final, 16)
        tmp = norm_tmp("z")
        yT = k.tile([128, 16, TW], F32, "zyT")
        st = [k.tile([128, TW], F32, "zst%d" % i) for i in range(3)]
        n = 0
        for ti in range(NT):
            rms_norm(src, gcol, 0, lambda c, t_: (yT.ap[:, c, :], yT.b), ti * TW, 1, P[7], tmp)
            for s in range(4):
                for cg in range(4):
                    ps = P[n % 4]
                    stt = st[n % 3]
                    n += 1
                    for ci in range(4):
                        c = 4 * cg + ci
                        k.op("pe", lambda: nc.tensor.transpose(out=ps.ap[:, ci * 128:(ci + 1) * 128], in_=yT.ap[:, c, s * 128:(s + 1) * 128],
                                                               identity=ident), r=[yT.b, cf.b], w=[ps.b], inc=(ci == 3))
                    k.op("act", lambda: nc.scalar.copy(out=stt.ap, in_=ps.ap), r=[ps.b], w=[stt.b])
                    r0 = ti * TW + s * 128
                    k.dma("sp", out[r0:r0 + 128, cg * 512:(cg + 1) * 512], stt.ap, r=[stt.b], w=[out_b])
        k.release(m0)

    phases = [
        ("tin", phase_transpose_in),
        ("mix0", phase_mixer0),
        ("ffn0", lambda: phase_ffn(0, xT[1], xT[2])),
        ("attn", lambda: phase_attn(xT[2], xT[3])),
        ("ffn1", lambda: phase_ffn(1, xT[3], xT[4])),
        ("final", lambda: phase_final(xT[4])),
    ]
    run = build_program.run_phases
    for name, fn in phases:
        if run is None or name in run:
            fn()
    fin = [out_b] + [t.b for t in xT] + [oT_d.b] + [t.b for t in dbg_t.values()]
    k.finish(fin)
    return nc, k


build_program.run_phases = None

WEIGHT_KEYS = ["ln_mix_even", "w_in", "conv_w", "conv_b", "conv_ln_g", "conv_ln_b", "ssm_lam_re", "ssm_lam_im", "ssm_log_step",
               "ssm_b_re", "ssm_b_im", "ssm_c_re", "ssm_c_im", "ssm_d", "ssm_w_glu", "ssm_b_glu", "w_out_even", "ln_mix_odd",
               "w_qkv", "w_o", "ln_ffn", "ffn_w_up", "ffn_conv_w", "ffn_w_down", "ln_final"]


def kernel(**inputs):
    nc, _ = build_program()
    x = np.ascontiguousarray(np.asarray(inputs["x"], dtype=np.float32))
    consts = make_consts()
    shared = {kk: np.ascontiguousarray(np.asarray(inputs[kk], dtype=np.float32)) for kk in WEIGHT_KEYS}
    in_maps = []
    for b in range(8):
        m = dict(shared)
        m["x"] = x[b]
        m["consts"] = consts
        in_maps.append(m)
    res = run_bass_kernel_spmd(nc, in_maps, core_ids=list(range(8)))
    return np.stack([np.asarray(r["out"], dtype=np.float32) for r in res.results], axis=0)
```
